# Optimizing a Trainium2 kernel written in Bass

```python
import jax, jax.numpy as jnp
from jax import lax
import numpy as np

D_MODEL = 1024
BATCH = 2
SEQ = 8192
DEPTH = 2

CTX_LEN = 256
GRID_W = 64
Q_BLOCK = 128
ROPE_THETA = 10000.0
NORM_EPS = 1e-6
N_MIXERS = 2

GQA_HEADS = 16
GQA_KV_HEADS = 4
GQA_GROUP = GQA_HEADS // GQA_KV_HEADS
GQA_HEAD_DIM = 64
GQA_WIDTH = GQA_HEADS * GQA_HEAD_DIM
GQA_KV_WIDTH = GQA_KV_HEADS * GQA_HEAD_DIM
GQA_IN_COLS = 2 * GQA_KV_WIDTH + 2 * GQA_WIDTH
GQA_CTX_COLS = 2 * GQA_KV_WIDTH

MLA_HEADS = 16
MLA_NOPE = 64
MLA_ROPE = 32
MLA_V = 64
MLA_Q_LORA = 384
MLA_KV_LORA = 256
MLA_WIDTH = MLA_HEADS * MLA_V
MLA_CTX_COLS = MLA_KV_LORA + MLA_ROPE
MLA_IN_COLS = MLA_CTX_COLS + MLA_Q_LORA + MLA_WIDTH

kernel_name = "hybrid_gqa_mla_diffusion_block"


def rms_norm(x, w):
    xf = x.astype(jnp.float32)
    y = xf * lax.rsqrt(jnp.mean(xf * xf, axis=-1, keepdims=True) + NORM_EPS)
    return (y * w.astype(jnp.float32)).astype(x.dtype)


def adaln_params(cond, w_mod, b_mod):
    mod = jax.nn.silu(cond) @ w_mod + b_mod
    return jnp.split(mod, 3, axis=-1)


def axial_rope_tables(n_tokens, rot_dim, dtype):
    rows = n_tokens // GRID_W
    row = jnp.repeat(jnp.arange(rows, dtype=jnp.float32), GRID_W)
    col = jnp.tile(jnp.arange(GRID_W, dtype=jnp.float32), rows)
    axis_dim = rot_dim // 2
    inv_freq = jnp.power(ROPE_THETA, -jnp.arange(0, axis_dim, 2, dtype=jnp.float32) / axis_dim)
    ang_r = row[:, None] * inv_freq[None, :]
    ang_c = col[:, None] * inv_freq[None, :]
    ang = jnp.concatenate([ang_r, ang_r, ang_c, ang_c], axis=-1)
    return jnp.cos(ang).astype(dtype), jnp.sin(ang).astype(dtype)


def apply_axial_rope(x, cos, sin):
    shape = (1, cos.shape[0]) + (1,) * (x.ndim - 3) + (cos.shape[-1],)
    cos = cos.reshape(shape)
    sin = sin.reshape(shape)
    x1, x2, x3, x4 = jnp.split(x, 4, axis=-1)
    rotated = jnp.concatenate([-x2, x1, -x4, x3], axis=-1)
    return x * cos + rotated * sin


def sweep_query_blocks(fn, *q_arrays):
    B, S = q_arrays[0].shape[:2]
    nb = S // Q_BLOCK
    blocks = tuple(jnp.moveaxis(a.reshape((B, nb, Q_BLOCK) + a.shape[2:]), 1, 0) for a in q_arrays)
    out = lax.map(lambda blk: fn(*blk), blocks)
    out = jnp.moveaxis(out, 0, 1)
    return out.reshape((B, S) + out.shape[3:])


def gqa_attend(q, k, v):
    s = jnp.einsum('bqhgd,bkhd->bhgqk', q, k).astype(jnp.float32) * (GQA_HEAD_DIM ** -0.5)
    p = jax.nn.softmax(s, axis=-1).astype(v.dtype)
    return jnp.einsum('bhgqk,bkhd->bqhgd', p, v)


def mla_attend(q_nope, q_rope, k_nope, k_rope, v):
    s = (jnp.einsum('bqhd,bkhd->bhqk', q_nope, k_nope)
         + jnp.einsum('bqhr,bkr->bhqk', q_rope, k_rope)).astype(jnp.float32) * ((MLA_NOPE + MLA_ROPE) ** -0.5)
    p = jax.nn.softmax(s, axis=-1).astype(v.dtype)
    return jnp.einsum('bhqk,bkhd->bqhd', p, v)


def gqa_mixer(h, hc, p, cos, sin, with_ctx_out):
    B, S, _ = h.shape
    L = hc.shape[1]
    cuts = [GQA_KV_WIDTH, 2 * GQA_KV_WIDTH, 2 * GQA_KV_WIDTH + GQA_WIDTH]
    k, v, q, g = jnp.split(h @ p["w_in"], cuts, axis=-1)
    q = apply_axial_rope(rms_norm(q.reshape(B, S, GQA_KV_HEADS, GQA_GROUP, GQA_HEAD_DIM), p["q_norm"]), cos, sin)
    k = apply_axial_rope(rms_norm(k.reshape(B, S, GQA_KV_HEADS, GQA_HEAD_DIM), p["k_norm"]), cos, sin)
    v = v.reshape(B, S, GQA_KV_HEADS, GQA_HEAD_DIM)
    w_ctx = p["w_in"] if with_ctx_out else p["w_in"][:, :GQA_CTX_COLS]
    proj_c = hc @ w_ctx
    kc = rms_norm(proj_c[..., :GQA_KV_WIDTH].reshape(B, L, GQA_KV_HEADS, GQA_HEAD_DIM), p["k_norm"])
    vc = proj_c[..., GQA_KV_WIDTH:GQA_CTX_COLS].reshape(B, L, GQA_KV_HEADS, GQA_HEAD_DIM)
    k_all = jnp.concatenate([kc, k], axis=1)
    v_all = jnp.concatenate([vc, v], axis=1)
    o = sweep_query_blocks(lambda qb: gqa_attend(qb, k_all, v_all), q).reshape(B, S, GQA_WIDTH)
    y = (o * jax.nn.silu(g)) @ p["w_out"]
    yc = None
    if with_ctx_out:
        qc = rms_norm(proj_c[..., GQA_CTX_COLS:GQA_CTX_COLS + GQA_WIDTH]
                      .reshape(B, L, GQA_KV_HEADS, GQA_GROUP, GQA_HEAD_DIM), p["q_norm"])
        gc = proj_c[..., GQA_CTX_COLS + GQA_WIDTH:]
        oc = gqa_attend(qc, kc, vc).reshape(B, L, GQA_WIDTH)
        yc = (oc * jax.nn.silu(gc)) @ p["w_out"]
    return y, yc


def mla_mixer(h, hc, p, cos, sin, with_ctx_out):
    B, S, _ = h.shape
    L = hc.shape[1]

    def kv_heads(kv_a, k_r, n):
        kv = (rms_norm(kv_a, p["kv_a_norm"]) @ p["w_kv_b"]).reshape(B, n, MLA_HEADS, MLA_NOPE + MLA_V)
        k_nope, v = jnp.split(kv, [MLA_NOPE], axis=-1)
        return rms_norm(k_nope, p["k_nope_norm"]), rms_norm(k_r, p["k_rope_norm"]), v

    def q_heads(q_a, n):
        q = (rms_norm(q_a, p["q_a_norm"]) @ p["w_q_b"]).reshape(B, n, MLA_HEADS, MLA_NOPE + MLA_ROPE)
        q = rms_norm(q, p["q_norm"])
        return jnp.split(q, [MLA_NOPE], axis=-1)

    cuts = [MLA_KV_LORA, MLA_CTX_COLS, MLA_CTX_COLS + MLA_Q_LORA]
    kv_a, k_r, q_a, g = jnp.split(h @ p["w_in"], cuts, axis=-1)
    k_nope, k_rope, v = kv_heads(kv_a, k_r, S)
    k_rope = apply_axial_rope(k_rope, cos, sin)
    q_nope, q_rope = q_heads(q_a, S)
    q_rope = apply_axial_rope(q_rope, cos, sin)
    w_ctx = p["w_in"] if with_ctx_out else p["w_in"][:, :MLA_CTX_COLS]
    proj_c = hc @ w_ctx
    kc_nope, kc_rope, vc = kv_heads(proj_c[..., :MLA_KV_LORA], proj_c[..., MLA_KV_LORA:MLA_CTX_COLS], L)
    kn_all = jnp.concatenate([kc_nope, k_nope], axis=1)
    kr_all = jnp.concatenate([kc_rope, k_rope], axis=1)
    v_all = jnp.concatenate([vc, v], axis=1)
    o = sweep_query_blocks(lambda qn, qr: mla_attend(qn, qr, kn_all, kr_all, v_all), q_nope, q_rope)
    y = (o.reshape(B, S, MLA_WIDTH) * jax.nn.silu(g)) @ p["w_out"]
    yc = None
    if with_ctx_out:
        qc_nope, qc_rope = q_heads(proj_c[..., MLA_CTX_COLS:MLA_CTX_COLS + MLA_Q_LORA], L)
        gc = proj_c[..., MLA_CTX_COLS + MLA_Q_LORA:]
        oc = mla_attend(qc_nope, qc_rope, kc_nope, kc_rope, vc).reshape(B, L, MLA_WIDTH)
        yc = (oc * jax.nn.silu(gc)) @ p["w_out"]
    return y, yc


def hybrid_layer(x, ctx, c, c_ctx, w_mod, b_mod, norm_w, mixer, mixer_params, cos, sin, with_ctx_out):
    shift, scale, gate = adaln_params(c, w_mod, b_mod)
    shift_c, scale_c, gate_c = adaln_params(c_ctx, w_mod, b_mod)
    h = rms_norm(x, norm_w) * (1.0 + scale[:, None, :]) + shift[:, None, :]
    hc = rms_norm(ctx, norm_w) * (1.0 + scale_c) + shift_c
    y, yc = mixer(h, hc, mixer_params, cos, sin, with_ctx_out)
    x = x + gate[:, None, :] * y
    if with_ctx_out:
        ctx = ctx + gate_c * yc
    return x, ctx


def _normal(key, shape, scale):
    return jax.random.normal(key, shape, dtype=jnp.float32) * scale


def _gain(key, n):
    return 1.0 + 0.01 * jax.random.normal(key, (n,), dtype=jnp.float32)


def setup_inputs(seed: int = 0) -> dict:
    key = jax.random.key(seed)
    ks = jax.random.split(key, 24)
    D = D_MODEL
    return {
        "x": _normal(ks[0], (BATCH, SEQ, D), 1.0),
        "c": _normal(ks[1], (BATCH, D), 1.0),
        "ctx": _normal(ks[2], (BATCH, CTX_LEN, D), 1.0),
        "c_ctx": _normal(ks[3], (D,), 1.0),
        "l0_w_mod": _normal(ks[4], (D, 3 * D), 0.5 * D ** -0.5),
        "l0_b_mod": _normal(ks[5], (3 * D,), 0.01),
        "l0_norm": _gain(ks[6], D),
        "l0_w_in": _normal(ks[7], (D, GQA_IN_COLS), D ** -0.5),
        "l0_q_norm": _gain(ks[8], GQA_HEAD_DIM),
        "l0_k_norm": _gain(ks[9], GQA_HEAD_DIM),
        "l0_w_out": _normal(ks[10], (GQA_WIDTH, D), GQA_WIDTH ** -0.5),
        "l1_w_mod": _normal(ks[11], (D, 3 * D), 0.5 * D ** -0.5),
        "l1_b_mod": _normal(ks[12], (3 * D,), 0.01),
        "l1_norm": _gain(ks[13], D),
        "l1_w_in": _normal(ks[14], (D, MLA_IN_COLS), D ** -0.5),
        "l1_kv_a_norm": _gain(ks[15], MLA_KV_LORA),
        "l1_w_kv_b": _normal(ks[16], (MLA_KV_LORA, MLA_HEADS * (MLA_NOPE + MLA_V)), MLA_KV_LORA ** -0.5),
        "l1_q_a_norm": _gain(ks[17], MLA_Q_LORA),
        "l1_w_q_b": _normal(ks[18], (MLA_Q_LORA, MLA_HEADS * (MLA_NOPE + MLA_ROPE)), MLA_Q_LORA ** -0.5),
        "l1_q_norm": _gain(ks[19], MLA_NOPE + MLA_ROPE),
        "l1_k_nope_norm": _gain(ks[20], MLA_NOPE),
        "l1_k_rope_norm": _gain(ks[21], MLA_ROPE),
        "l1_w_out": _normal(ks[22], (MLA_WIDTH, D), MLA_WIDTH ** -0.5),
    }


def reference(x, c, ctx, c_ctx,
              l0_w_mod, l0_b_mod, l0_norm, l0_w_in, l0_q_norm, l0_k_norm, l0_w_out,
              l1_w_mod, l1_b_mod, l1_norm, l1_w_in, l1_kv_a_norm, l1_w_kv_b, l1_q_a_norm, l1_w_q_b,
              l1_q_norm, l1_k_nope_norm, l1_k_rope_norm, l1_w_out):
    n_tokens = x.shape[1]
    cos_a, sin_a = axial_rope_tables(n_tokens, GQA_HEAD_DIM, x.dtype)
    cos_b, sin_b = axial_rope_tables(n_tokens, MLA_ROPE, x.dtype)
    gqa_params = {"w_in": l0_w_in, "q_norm": l0_q_norm, "k_norm": l0_k_norm, "w_out": l0_w_out}
    mla_params = {"w_in": l1_w_in, "kv_a_norm": l1_kv_a_norm, "w_kv_b": l1_w_kv_b,
                  "q_a_norm": l1_q_a_norm, "w_q_b": l1_w_q_b, "q_norm": l1_q_norm,
                  "k_nope_norm": l1_k_nope_norm, "k_rope_norm": l1_k_rope_norm, "w_out": l1_w_out}
    layers = [
        (l0_w_mod, l0_b_mod, l0_norm, gqa_mixer, gqa_params, cos_a, sin_a),
        (l1_w_mod, l1_b_mod, l1_norm, mla_mixer, mla_params, cos_b, sin_b),
    ]
    for i in range(DEPTH):
        w_mod, b_mod, norm_w, mixer, params, cos, sin = layers[i]
        x, ctx = hybrid_layer(x, ctx, c, c_ctx, w_mod, b_mod, norm_w, mixer, params, cos, sin,
                              with_ctx_out=(i < DEPTH - 1))
    return x
```

```python
from contextlib import ExitStack
import numpy as np
import ml_dtypes
import concourse.bass as bass
import concourse.mybir as mybir
from concourse.bass_utils import run_bass_kernel_spmd

F32 = mybir.dt.float32
BF16 = mybir.dt.bfloat16
AF = mybir.ActivationFunctionType
ALU = mybir.AluOpType
AX = mybir.AxisListType

D = 1024
NCH = 8
TOK = 2048
NT = 16
CTX = 256
NKEY = CTX + 4 * TOK
NKT = NKEY // 128
EPS = 1e-6
GROUPS = [[0, 1, 2, 3], [4, 5, 6, 7]]

R_G0, R_G1, R_QN0, R_KN0, R_KVA, R_KRN, R_QAN, R_QN1, R_KNN = 0, 1024, 2048, 2112, 2176, 2432, 2464, 2848, 2944
NREP = 3008
V_CB, V_CC, V_N0, V_B0, V_N1, V_B1 = 0, 8, 16, 24, 48, 56
NVEC = 80


class Trk:
    __slots__ = ("w", "r", "ep")

    def __init__(self):
        self.w = {}
        self.r = {}
        self.ep = -1


class Prog:
    ENG = ("pe", "act", "dve", "pool", "sp")

    def __init__(self, nc, es):
        self.nc = nc
        self.q = {e: [] for e in self.ENG}
        self.sem = {e: es.enter_context(nc.semaphore("s_" + e)) for e in ("pe", "act", "dve", "pool")}
        self.cnt = {e: 0 for e in self.sem}
        self.seen = {e: {} for e in self.ENG}
        self.semobj = dict(self.sem)
        self.epoch = 0
        K = 6
        self.ring = {}
        for qn in ("sp", "pool"):
            self.ring[qn] = []
            for i in range(K):
                sm = es.enter_context(nc.semaphore(f"d_{qn}{i}"))
                self.ring[qn].append(sm)
                self.semobj[f"d_{qn}{i}"] = sm
        self.ringn = {qn: 0 for qn in self.ring}
        self.ccsem = es.enter_context(nc.semaphore("s_cc"))
        self.semobj["cc"] = self.ccsem
        self.ccn = 0

    def _tr(self, t):
        if t.ep != self.epoch:
            t.w = {}
            t.r = {}
            t.ep = self.epoch
        return t

    def _waits(self, eng, rd, wr):
        need = {}
        for t in rd:
            t = self._tr(t)
            for k, v in t.w.items():
                if k == eng and eng == "pe":
                    continue
                if need.get(k, 0) < v:
                    need[k] = v
        for t in wr:
            t = self._tr(t)
            for dct in (t.w, t.r):
                for k, v in dct.items():
                    if k == eng:
                        continue
                    if need.get(k, 0) < v:
                        need[k] = v
        out = []
        sn = self.seen[eng]
        for k, v in need.items():
            if sn.get(k, 0) < v:
                sn[k] = v
                out.append((self.semobj[k], v))
        return out

    def _mark(self, key, val, rd, wr):
        for t in rd:
            t = self._tr(t)
            if t.r.get(key, 0) < val:
                t.r[key] = val
        for t in wr:
            t = self._tr(t)
            if t.w.get(key, 0) < val:
                t.w[key] = val

    def op(self, eng, fn, rd=(), wr=(), inc=True):
        waits = self._waits(eng, rd, wr)
        if inc:
            self.cnt[eng] += 1
            val = self.cnt[eng]
        else:
            val = self.cnt[eng] + 1
        sem = self.sem[eng]

        def emit(h):
            for (sm, v) in waits:
                h.wait_ge(sm, v)
            ins = fn(h)
            if inc:
                ins.then_inc(sem, 1)
        self.q[eng].append(emit)
        self._mark(eng, val, rd, wr)

    def dma(self, qn, out, in_, rd=(), wr=(), **kw):
        waits = self._waits(qn, rd, wr)
        n = self.ringn[qn]
        self.ringn[qn] += 1
        K = len(self.ring[qn])
        slot, gen = n % K, n // K
        sem = self.ring[qn][slot]
        key = f"d_{qn}{slot}"
        prev, val = 16 * gen, 16 * (gen + 1)
        need_prev = gen > 0 and self.seen[qn].get(key, 0) < prev
        if need_prev:
            self.seen[qn][key] = prev

        def emit(h):
            for (sm, v) in waits:
                h.wait_ge(sm, v)
            if need_prev:
                h.wait_ge(sem, prev)
            h.dma_start(out=out, in_=in_, **kw).then_inc(sem, 16)
        self.q[qn].append(emit)
        self._mark(key, val, rd, wr)

    def allgather(self, src, dst, rd=(), wr=()):
        waits = self._waits("pool", rd, wr)
        self.ccn += 1
        val = self.ccn
        sem = self.ccsem

        def emit(h):
            for (sm, v) in waits:
                h.wait_ge(sm, v)
            h.collective_compute("AllGather", ALU.bypass, replica_groups=GROUPS,
                                 ins=[src.ap().opt()], outs=[dst.ap().opt()]).then_inc(sem)
        self.q["pool"].append(emit)
        self._mark("cc", val, rd, wr)

    def flush(self, final_waits=()):
        for qn in self.ring:
            n = self.ringn[qn]
            K = len(self.ring[qn])
            for slot in range(K):
                cntslot = (n - slot + K - 1) // K
                if cntslot > 0 and self.seen[qn].get(f"d_{qn}{slot}", 0) < 16 * cntslot:
                    self.seen[qn][f"d_{qn}{slot}"] = 16 * cntslot
                    sem = self.ring[qn][slot]
                    self.q[qn].append(lambda h, sem=sem, v=16 * cntslot: h.wait_ge(sem, v))
        if self.ccn > 0 and self.seen["pool"].get("cc", 0) < self.ccn:
            self.seen["pool"]["cc"] = self.ccn
            self.q["pool"].append(lambda h, v=self.ccn: h.wait_ge(self.ccsem, v))
        q = self.q
        self.q = {e: [] for e in self.ENG}
        with self.nc.Block() as block:
            @block.tensor
            def _(e):
                for f in q["pe"]:
                    f(e)

            @block.scalar
            def _(e):
                for f in q["act"]:
                    f(e)

            @block.vector
            def _(e):
                for f in q["dve"]:
                    f(e)

            @block.gpsimd
            def _(e):
                for f in q["pool"]:
                    f(e)

            @block.sync
            def _(e):
                for f in q["sp"]:
                    f(e)
        self.epoch += 1


def bc(ap, shape, axis):
    return ap.unsqueeze(axis).to_broadcast(list(shape))


def build(layers=(0, 1), dbg=False, stop=None):
    nc = bass.Bass("TRN2", target_bir_lowering=False)
    dt_in = lambda name, shape, dt=F32: nc.dram_tensor(name, list(shape), dt, kind="ExternalInput")
    x_own = dt_in("x_own", [TOK, D])
    ctx_b = dt_in("ctx_b", [CTX, D])
    vecT = dt_in("vecT", [128, NVEC])
    rep = dt_in("rep", [128, NREP])
    rope0 = dt_in("rope0", [128, NT, 128])
    rope1 = dt_in("rope1", [128, NT, 64])
    ident_d = dt_in("ident", [128, 128], BF16)
    identf_d = dt_in("identf", [128, 128])
    w_mod = [dt_in("l0_w_mod", [D, 3 * D]), dt_in("l1_w_mod", [D, 3 * D])]
    w_in = [dt_in("l0_w_in", [D, 2560]), dt_in("l1_w_in", [D, 1696])]
    w_out = [dt_in("l0_w_out", [D, D]), dt_in("l1_w_out", [D, D])]
    w_kvb = dt_in("l1_w_kv_b", [256, 2048])
    w_qb = dt_in("l1_w_q_b", [384, 1536])
    out_d = nc.dram_tensor("out", [TOK, D], F32, kind="ExternalOutput")
    okind = dict(kind="ExternalOutput") if (dbg or 1 not in layers) else {}
    if 0 not in layers:
        okind = dict(kind="ExternalInput")
    x1s = nc.dram_tensor("x1s", [TOK, D], F32, **okind)
    ctx1s = nc.dram_tensor("ctx1s", [CTX, D], F32, **okind)
    payK0 = nc.dram_tensor("payK0", [256, TOK], BF16)
    payKG0 = nc.dram_tensor("payKG0", [1024, TOK], BF16)
    payV0 = nc.dram_tensor("payV0", [TOK, 256], BF16)
    payVG0 = nc.dram_tensor("payVG0", [4 * TOK, 256], BF16)
    paycK0 = nc.dram_tensor("paycK0", [256, CTX], BF16)
    paycV0 = nc.dram_tensor("paycV0", [CTX, 256], BF16)
    pay1 = nc.dram_tensor("pay1", [256, TOK], BF16)
    payG1 = nc.dram_tensor("payG1", [4 * 256, TOK], BF16)
    pay1r = nc.dram_tensor("pay1r", [32, TOK], BF16)
    payG1r = nc.dram_tensor("payG1r", [4 * 32, TOK], BF16)
    payc1 = nc.dram_tensor("payc1", [288, CTX], BF16)
    PRE = (0 in layers) and (1 in layers)
    wmod1_bf = nc.dram_tensor("wmod1_bf", [D, 3 * D], BF16)
    win1_bf = nc.dram_tensor("win1_bf", [D, 1696], BF16)
    wout1_bf = nc.dram_tensor("wout1_bf", [D, D], BF16)
    wkvb_bf = nc.dram_tensor("wkvb_bf", [256, 2048], BF16)
    wqb_bf = nc.dram_tensor("wqb_bf", [384, 1536], BF16)

    _names = {}

    def _un(n):
        _names[n] = _names.get(n, 0) + 1
        return f"{n}_{_names[n]}"

    es = ExitStack()
    with es:
        P = Prog(nc, es)

        def sb(stack, name, shape, dt):
            return stack.enter_context(nc.sbuf_tensor(_un("s_" + name), list(shape), dt))

        def ps(stack, name, shape, dt):
            return stack.enter_context(nc.psum_tensor(_un("p_" + name), list(shape), dt))

        vec_sb = sb(es, "vec_sb", [128, NVEC], F32)
        rep_sb = sb(es, "rep_sb", [128, NREP], F32)
        ident = sb(es, "ident", [128, 128], BF16)
        identf = sb(es, "identf", [128, 128], F32)
        modT = [sb(es, f"modT{l}", [128, 24, 2], F32) for l in range(2)]
        Amod = [sb(es, f"Amod{l}", [128, 8, 2], F32) for l in range(2)]
        gate = [sb(es, "gate0", [128, 2, D], F32), sb(es, "gate1", [128, 1, D], F32)]
        t_const = Trk()

        scT = sb(es, "scT", [128, NCH, 2], BF16)
        t_sc = Trk()

        def load_cast(dst_fn, src_d, rows, cols, t_dst, stage, t_stage, piece=1024):
            k = 0
            for r0 in range(0, rows, 128):
                for cb in range(0, cols, piece):
                    ce = min(cols, cb + piece)
                    dst = dst_fn(r0 // 128, cb, ce)
                    if k % 2 == 0 or not stage:
                        P.dma("pool", dst, src_d[r0:r0 + 128, cb:ce], wr=[t_dst])
                    else:
                        si = (k // 2) % len(stage)
                        sg = stage[si][:, 0:ce - cb]
                        P.dma("sp", sg, src_d[r0:r0 + 128, cb:ce], wr=[t_stage[si]])
                        if (k // 2) % 2 == 0:
                            P.op("act", lambda h, dst=dst, sg=sg: h.activation(out=dst, in_=sg, func=AF.Copy),
                                 rd=[t_stage[si]], wr=[t_dst])
                        else:
                            P.op("dve", lambda h, dst=dst, sg=sg: h.tensor_copy(out=dst, in_=sg), rd=[t_stage[si]], wr=[t_dst])
                    k += 1

        def adaln(l, stk):
            wm = sb(stk, "wm", [128, NCH, 3 * D], BF16)
            screp = sb(stk, "screp", [128, NCH, 2, 128], BF16)
            tmpm = sb(stk, "tmpm", [128, 8, 2], F32)
            modps = ps(stk, "modps", [128, 24, 2], F32)
            gps0 = ps(stk, "gps0", [128, 512], F32)
            gps = [gps0, gps0]
            t_wm, t_mod, t_tmp, t_ml = Trk(), Trk(), Trk(), Trk()
            t_g0 = Trk()
            t_gps = [t_g0, t_g0]

            def load():
                if l == 1 and PRE:
                    for c in range(NCH):
                        P.dma("sp", wm[:, c, :], wmod1_bf[c * 128:(c + 1) * 128, :], wr=[t_wm])
                else:
                    stg = [sb(stk, f"wstg{i}", [128, 1024], F32) for i in range(3)]
                    load_cast(lambda c, cb, ce: wm[:, c, cb:ce], w_mod[l], D, 3 * D, t_wm, stg, [Trk() for _ in stg])

            def compute():
                P.op("dve", lambda h: h.tensor_copy(out=screp[:, :, :, :], in_=bc(scT[:, :, :], [128, NCH, 2, 128], 3)),
                     rd=[t_sc], wr=[t_wm])
                for j in range(24):
                    for c in range(NCH):
                        P.op("pe", lambda h, j=j, c=c: h.matmul(modps[:, j, :], wm[:, c, j * 128:(j + 1) * 128],
                                                                 scT[:, c, :], start=(c == 0), stop=(c == NCH - 1)),
                             rd=[t_wm, t_sc], wr=[t_mod], inc=(c == NCH - 1))
                vb = V_B0 if l == 0 else V_B1
                vn = V_N0 if l == 0 else V_N1
                P.op("dve", lambda h: h.tensor_tensor(out=modT[l][:, :, :], in0=modps[:, :, :],
                                                      in1=bc(vec_sb[:, vb:vb + 24], [128, 24, 2], 2), op=ALU.add),
                     rd=[t_mod, t_const], wr=[t_ml, t_const])
                P.op("dve", lambda h: h.tensor_scalar(out=tmpm[:, :, :], in0=modT[l][:, 8:16, :], scalar1=1.0,
                                                      scalar2=None, op0=ALU.add), rd=[t_ml], wr=[t_tmp])
                P.op("dve", lambda h: h.tensor_tensor(out=Amod[l][:, :, :], in0=tmpm[:, :, :],
                                                      in1=bc(vec_sb[:, vn:vn + 8], [128, 8, 2], 2), op=ALU.mult),
                     rd=[t_tmp, t_const], wr=[t_const])
                for t in range(2 if l == 0 else 1):
                    for half in range(2):
                        g = gps[half]
                        for c in range(NCH):
                            P.op("pe", lambda h, g=g, c=c, t=t, half=half: h.matmul(
                                g[:, :], screp[:, c, t, :], wm[:, c, 2 * D + half * 512:2 * D + (half + 1) * 512],
                                start=(c == 0), stop=(c == NCH - 1)),
                                rd=[t_wm, t_sc], wr=[t_gps[half]], inc=(c == NCH - 1))
                        ro = R_G0 if l == 0 else R_G1
                        P.op("dve", lambda h, g=g, t=t, half=half, ro=ro: h.tensor_tensor(
                            out=gate[l][:, t, half * 512:(half + 1) * 512], in0=g[:, :],
                            in1=rep_sb[:, ro + half * 512:ro + (half + 1) * 512], op=ALU.add),
                            rd=[t_gps[half], t_const], wr=[t_const])
            return load, compute

        with ExitStack() as s0:
            P.dma("sp", vec_sb[:, :], vecT[:, :], wr=[t_const])
            P.dma("sp", rep_sb[:, :], rep[:, :], wr=[t_const])
            P.dma("sp", ident[:, :], ident_d[:, :], wr=[t_const])
            P.dma("sp", identf[:, :], identf_d[:, :], wr=[t_const])
            P.op("act", lambda h: h.activation(out=scT[:, :, :].rearrange("p c t -> p t c"),
                                               in_=vec_sb[:, V_CB:V_CB + 16].rearrange("p (t c) -> p t c", t=2),
                                               func=AF.Silu), rd=[t_const], wr=[t_sc])
            first = layers[0]
            ld_, cp_ = adaln(first, s0)
            ld_()
            cp_()
            P.flush()

        def gnorm(src, G, n, wcol, dst, scr, ssb, rsb, t_src, t_dst, t_scr, t_ss):
            s3 = src.rearrange("p (g n) -> p g n", g=G)
            d3 = dst.rearrange("p (g n) -> p g n", g=G)
            P.op("act", lambda h: h.activation(out=scr, in_=src, func=AF.Square), rd=[t_src], wr=[t_scr])
            P.op("dve", lambda h: h.tensor_reduce(out=ssb, in_=scr.rearrange("p (g n) -> p g n", g=G),
                                                  axis=AX.X, op=ALU.add), rd=[t_scr], wr=[t_ss])
            P.op("act", lambda h: h.activation(out=rsb, in_=ssb, func=AF.Sqrt, bias=epsb[:, 0:1], scale=1.0 / n),
                 rd=[t_ss, t_const], wr=[t_ss])
            P.op("dve", lambda h: h.reciprocal(out=rsb, in_=rsb), rd=[t_ss], wr=[t_ss])
            P.op("dve", lambda h: h.tensor_tensor(out=scr.rearrange("p (g n) -> p g n", g=G), in0=s3,
                                                  in1=bc(rsb, [128, G, n], 2), op=ALU.mult),
                 rd=[t_src, t_ss], wr=[t_scr])
            P.op("dve", lambda h: h.tensor_tensor(out=d3, in0=scr.rearrange("p (g n) -> p g n", g=G),
                                                  in1=bc(rep_sb[:, wcol:wcol + n], [128, G, n], 1), op=ALU.mult),
                 rd=[t_scr, t_const], wr=[t_dst])

        def rope(src, G, n, tab, dst5, t1, t2, t_src, t_tab, t_dst, t_t1, t_t2, off=0, rn=None):
            rn = rn or n
            q = rn // 4
            s3 = src.rearrange("p (g n) -> p g n", g=G)[:, :, off:off + rn]
            s5 = s3.rearrange("p g (a b q) -> p g a b q", a=2, b=2)
            cosb = bc(tab[:, 0:rn], [128, G, rn], 1)
            sin4 = tab[:, rn:2 * rn].rearrange("p (a b q) -> p a b q", a=2, b=2)
            a1 = t1[:, 0:G * rn].rearrange("p (g n) -> p g n", g=G)
            a2 = t2[:, 0:G * rn].rearrange("p (g a b q) -> p g a b q", g=G, a=2, b=2)
            P.op("dve", lambda h: h.tensor_tensor(out=a1, in0=s3, in1=cosb, op=ALU.mult),
                 rd=[t_src, t_tab], wr=[t_t1])
            for b in range(2):
                P.op("pool", lambda h, b=b: h.tensor_tensor(out=a2[:, :, :, b, :], in0=s5[:, :, :, 1 - b, :],
                                                            in1=bc(sin4[:, :, b, :], [128, G, 2, q], 1), op=ALU.mult),
                     rd=[t_src, t_tab], wr=[t_t2])
            a15 = t1[:, 0:G * rn].rearrange("p (g a b q) -> p g a b q", g=G, a=2, b=2)
            for a in range(2):
                P.op("dve", lambda h, a=a: h.tensor_tensor(out=dst5[:, :, a, :, :], in0=a15[:, :, a, :, :],
                                                           in1=a2[:, :, a, :, :], op=ALU.add),
                     rd=[t_t1, t_t2], wr=[t_dst])

        epsb = sb(es, "epsb", [128, 1], F32)
        P.op("dve", lambda h: h.memset(epsb[:, :], EPS), wr=[t_const])

        for L in layers:
            NQT = NT + 2 if L == 0 else NT
            NPT = NT + 2
            QC = NQT * 128
            WIN = 2560 if L == 0 else 1696
            with ExitStack() as sl:
                gsb = sb(sl, f"gsb{L}", [128, NQT, D], BF16)
                t_g = [Trk() for _ in range(NQT)]
                if L == 0:
                    QT = sb(sl, "QT", [128, 8, QC], BF16)
                else:
                    qanT = sb(sl, "qanT", [128, 3, QC], BF16)
                t_QT = Trk()
                t_pay = Trk()
                with ExitStack() as sp_:
                    RW = 128 if L == 0 else 64
                    win = sb(sp_, f"win{L}", [128, NCH, WIN], BF16)
                    ropet = [sb(sp_, f"rope{L}_{i}", [128, RW], F32) for i in range(3)]
                    xs = [sb(sp_, f"xs{i}", [128, D], F32) for i in range(2)]
                    scrF = sb(sp_, "scrF", [128, D], F32)
                    bscr = sb(sp_, "bscr", [128, D], F32)
                    kscr = sb(sp_, "kscr", [128, 512], F32)
                    t1b = sb(sp_, "t1b", [128, D], F32)
                    t2b = sb(sp_, "t2b", [128, D], F32)
                    t1k = sb(sp_, "t1k", [128, 256], F32)
                    t2k = sb(sp_, "t2k", [128, 256], F32)
                    nrmk = sb(sp_, "nrmk", [128, 256], F32)
                    qf2 = [sb(sp_, f"qf{i}", [128, D], F32) for i in range(2)]
                    kvf2 = [sb(sp_, f"kvf{i}", [128, 512], F32) for i in range(2)]
                    nrm = bscr
                    xn = sb(sp_, "xn", [128, D], BF16)
                    hT = [sb(sp_, f"hT{i}", [128, NCH, 128], BF16) for i in range(2)]
                    ssF = sb(sp_, "ssF", [128, 2], F32)
                    ssK = sb(sp_, "ssK", [128, 8], F32)
                    ssQ = sb(sp_, "ssQ", [128, 32], F32)
                    qrb = sb(sp_, "qrb", [128, D], BF16)
                    krb = sb(sp_, "krb", [128, 512], BF16)
                    tp = ps(sp_, "tp", [128, NCH, 128], BF16)
                    tp2 = ps(sp_, "tp2", [128, NCH, 128], BF16)
                    tp2k = ps(sp_, "tp2k", [128, NCH, 128], BF16)
                    ps_kv = ps(sp_, "ps_kv", [128, 512], F32)
                    ps_q = ps(sp_, "ps_q", [128, D], F32)
                    ps_g = ps(sp_, "ps_g", [128, D], F32)
                    t_win = Trk()
                    t_rope, t_xs, t_hT = [Trk(), Trk(), Trk()], [Trk(), Trk()], [Trk(), Trk()]
                    t_scrF, t_ssF, t_xn, t_tp, t_pkv, t_pq, t_pg = (Trk() for _ in range(7))
                    t_qf2, t_kvf2 = [Trk(), Trk()], [Trk(), Trk()]
                    t_bscr, t_kscr, t_t1, t_t2, t_t1k, t_t2k, t_nrmk, t_ssK, t_ssQ = (Trk() for _ in range(9))
                    t_nrm = t_bscr
                    t_qrb, t_krb, t_tp2, t_tp2k = (Trk() for _ in range(4))
                    t_t1q, t_t2q = [Trk() for _ in range(4)], [Trk() for _ in range(4)]
                    if L == 0:
                        kst = sb(sp_, "kst", [128, 2, TOK], BF16)
                        vst = sb(sp_, "vst", [128, NT, 256], BF16)
                        kstc = sb(sp_, "kstc", [128, 2, CTX], BF16)
                        vstc = sb(sp_, "vstc", [128, 2, 256], BF16)
                    else:
                        kst = sb(sp_, "kst1", [128, 3, TOK], BF16)
                        kstc = sb(sp_, "kstc1", [128, 3, CTX], BF16)
                    t_kst = Trk()
                    for c in range(NCH):
                        if L == 1 and PRE:
                            P.dma("sp", win[:, c, :], win1_bf[c * 128:(c + 1) * 128, :], wr=[t_win])
                            continue
                        pass
                    if not (L == 1 and PRE):
                        load_cast(lambda c, cb, ce: win[:, c, cb:ce], w_in[L], D, WIN, t_win, [bscr, scrF], [t_bscr, t_scrF])
                    rope_d = rope0 if L == 0 else rope1

                    def pjam(gens):
                        gens = list(gens)
                        while gens:
                            for g_ in list(gens):
                                try:
                                    next(g_)
                                except StopIteration:
                                    gens.remove(g_)

                    def gnorm_g(src, G, n, wcol, dst, scr_, ss_, rs_, t_src, t_dst, t_scr_, t_ss_):
                        s3 = src.rearrange("p (g n) -> p g n", g=G)
                        d3 = dst.rearrange("p (g n) -> p g n", g=G)
                        c3 = scr_.rearrange("p (g n) -> p g n", g=G)
                        P.op("act", lambda h: h.activation(out=scr_, in_=src, func=AF.Square), rd=[t_src], wr=[t_scr_])
                        yield
                        P.op("dve", lambda h: h.tensor_reduce(out=ss_, in_=c3, axis=AX.X, op=ALU.add), rd=[t_scr_], wr=[t_ss_])
                        yield
                        P.op("act", lambda h: h.activation(out=rs_, in_=ss_, func=AF.Sqrt, bias=epsb[:, 0:1], scale=1.0 / n),
                             rd=[t_ss_, t_const], wr=[t_ss_])
                        yield
                        P.op("dve", lambda h: h.reciprocal(out=rs_, in_=rs_), rd=[t_ss_], wr=[t_ss_])
                        yield
                        big = G * n >= 1024
                        if big:
                            hg = G // 2
                            P.op("pool", lambda h: h.tensor_tensor(out=c3[:, 0:hg, :], in0=s3[:, 0:hg, :],
                                                                   in1=bc(rs_[:, 0:hg], [128, hg, n], 2), op=ALU.mult),
                                 rd=[t_src, t_ss_], wr=[t_scr_])
                            P.op("dve", lambda h: h.tensor_tensor(out=c3[:, hg:G, :], in0=s3[:, hg:G, :],
                                                                  in1=bc(rs_[:, hg:G], [128, G - hg, n], 2), op=ALU.mult),
                                 rd=[t_src, t_ss_], wr=[t_scr_])
                        else:
                            P.op("dve", lambda h: h.tensor_tensor(out=c3, in0=s3, in1=bc(rs_, [128, G, n], 2), op=ALU.mult),
                                 rd=[t_src, t_ss_], wr=[t_scr_])
                        yield
                        P.op("pool", lambda h: h.tensor_tensor(out=d3, in0=c3, in1=bc(rep_sb[:, wcol:wcol + n], [128, G, n], 1),
                                                               op=ALU.mult), rd=[t_scr_, t_const], wr=[t_dst])
                        yield

                    def front1(ti):
                        isctx = ti >= NT
                        cond = 1 if isctx else 0
                        s = ti % 2
                        if L == 0:
                            src = ctx_b[(ti - NT) * 128:(ti - NT + 1) * 128, :] if isctx else x_own[ti * 128:(ti + 1) * 128, :]
                        else:
                            src = ctx1s[(ti - NT) * 128:(ti - NT + 1) * 128, :] if isctx else x1s[ti * 128:(ti + 1) * 128, :]
                        P.dma("sp", xs[s][:, :], src, wr=[t_xs[s]])
                        if not isctx:
                            P.dma("sp", ropet[ti % 3][:, :], rope_d[:, ti, :], wr=[t_rope[ti % 3]])
                        yield
                        P.op("act", lambda h: h.activation(out=scrF[:, :], in_=xs[s][:, :], func=AF.Square),
                             rd=[t_xs[s]], wr=[t_scrF])
                        yield
                        P.op("dve", lambda h: h.tensor_reduce(out=ssF[:, 0:1], in_=scrF[:, :], axis=AX.X, op=ALU.add),
                             rd=[t_scrF], wr=[t_ssF])
                        yield
                        P.op("act", lambda h: h.activation(out=ssF[:, 1:2], in_=ssF[:, 0:1], func=AF.Sqrt,
                                                           bias=epsb[:, 0:1], scale=1.0 / D), rd=[t_ssF, t_const], wr=[t_ssF])
                        yield
                        P.op("dve", lambda h: h.reciprocal(out=ssF[:, 1:2], in_=ssF[:, 1:2]), rd=[t_ssF], wr=[t_ssF])
                        yield
                        P.op("act", lambda h: h.activation(out=xn[:, :], in_=xs[s][:, :], func=AF.Copy, scale=ssF[:, 1:2]),
                             rd=[t_xs[s], t_ssF], wr=[t_xn])
                        yield
                        for c in range(NCH):
                            P.op("pe", lambda h, c=c: h.transpose(tp[:, c, :], xn[:, c * 128:(c + 1) * 128], ident[:, :]),
                                 rd=[t_xn, t_const], wr=[t_tp], inc=(c == NCH - 1))
                        yield
                        for c in range(NCH):
                            P.op("dve", lambda h, c=c: h.tensor_scalar(
                                out=hT[s][:, c, :], in0=tp[:, c, :], scalar1=Amod[L][:, c, cond:cond + 1],
                                scalar2=modT[L][:, c, cond:cond + 1], op0=ALU.mult, op1=ALU.add),
                                rd=[t_tp, t_const], wr=[t_hT[s]])
                            if c % 2 == 1:
                                yield

                    def front2(ti):
                        isctx = ti >= NT
                        s = ti % 2
                        qf, kvf, t_qf, t_kvf = qf2[s], kvf2[s], t_qf2[s], t_kvf2[s]

                        def proj(dst, c0, c1, t_dst):
                            for c in range(NCH):
                                P.op("pe", lambda h, c=c: h.matmul(dst, hT[s][:, c, :], win[:, c, c0:c1],
                                                                    start=(c == 0), stop=(c == NCH - 1)),
                                     rd=[t_hT[s], t_win], wr=[t_dst], inc=(c == NCH - 1))
                        need_q = (L == 0) or not isctx
                        if L == 0:
                            proj(ps_g[:, 0:512], 1536, 2048, t_pg)
                            proj(ps_g[:, 512:1024], 2048, 2560, t_pg)
                            yield
                            P.op("act", lambda h: h.activation(out=gsb[:, ti, :], in_=ps_g[:, :], func=AF.Silu),
                                 rd=[t_pg], wr=[t_g[ti]])
                            proj(ps_q[:, 0:512], 512, 1024, t_pq)
                            proj(ps_q[:, 512:1024], 1024, 1536, t_pq)
                            yield
                            P.op("act", lambda h: h.activation(out=qf[:, :], in_=ps_q[:, :], func=AF.Copy), rd=[t_pq], wr=[t_qf])
                            proj(ps_kv[:, 0:512], 0, 512, t_pkv)
                            yield
                            P.op("act", lambda h: h.activation(out=kvf[:, :], in_=ps_kv[:, :], func=AF.Copy), rd=[t_pkv], wr=[t_kvf])
                        else:
                            if need_q:
                                proj(ps_g[:, 0:512], 672, 1184, t_pg)
                                proj(ps_g[:, 512:1024], 1184, 1696, t_pg)
                                yield
                                P.op("act", lambda h: h.activation(out=gsb[:, ti, :], in_=ps_g[:, :], func=AF.Silu),
                                     rd=[t_pg], wr=[t_g[ti]])
                                proj(ps_q[:, 0:384], 288, 672, t_pq)
                                yield
                                P.op("act", lambda h: h.activation(out=qf[:, 0:384], in_=ps_q[:, 0:384], func=AF.Copy),
                                     rd=[t_pq], wr=[t_qf])
                            proj(ps_kv[:, 0:288], 0, 288, t_pkv)
                            yield
                            P.op("act", lambda h: h.activation(out=kvf[:, 0:288], in_=ps_kv[:, 0:288], func=AF.Copy),
                                 rd=[t_pkv], wr=[t_kvf])
                        yield

                    def backK(ti):
                        isctx = ti >= NT
                        s = ti % 2
                        kvf, t_kvf = kvf2[s], t_kvf2[s]
                        if L == 0:
                            yield from gnorm_g(kvf[:, 0:256], 4, 64, R_KN0, nrmk[:, 0:256], kscr[:, 0:256], ssK[:, 0:4], ssK[:, 4:8],
                                               t_kvf, t_nrmk, t_kscr, t_ssK)
                            if isctx:
                                P.op("dve", lambda h: h.tensor_copy(out=krb[:, 0:256], in_=nrmk[:, 0:256]), rd=[t_nrmk], wr=[t_krb])
                            else:
                                rope(nrmk[:, 0:256], 4, 64, ropet[ti % 3][:, :],
                                     krb[:, 0:256].rearrange("p (g a b q) -> p g a b q", g=4, a=2, b=2),
                                     t1k, t2k, t_nrmk, t_rope[ti % 3], t_krb, t_t1k, t_t2k)
                            yield
                            for j in range(2):
                                P.op("pe", lambda h, j=j: h.transpose(tp2k[:, j, :], krb[:, j * 128:(j + 1) * 128], ident[:, :]),
                                     rd=[t_krb, t_const], wr=[t_tp2k], inc=(j == 1))
                            yield
                            if isctx:
                                kd = kstc[:, :, (ti - NT) * 128:(ti - NT + 1) * 128]
                                vd = vstc[:, ti - NT, :]
                            else:
                                kd = kst[:, :, ti * 128:(ti + 1) * 128]
                                vd = vst[:, ti, :]
                            P.op("dve", lambda h: h.tensor_copy(out=kd, in_=tp2k[:, 0:2, :]), rd=[t_tp2k], wr=[t_kst])
                            P.op("pool", lambda h: h.tensor_copy(out=vd, in_=kvf[:, 256:512]), rd=[t_kvf], wr=[t_kst])
                            yield
                        else:
                            yield from gnorm_g(kvf[:, 0:256], 1, 256, R_KVA, krb[:, 0:256], kscr[:, 0:256], ssK[:, 0:1], ssK[:, 4:5],
                                               t_kvf, t_krb, t_kscr, t_ssK)
                            yield from gnorm_g(kvf[:, 256:288], 1, 32, R_KRN, nrmk[:, 0:32], kscr[:, 256:288], ssK[:, 1:2], ssK[:, 5:6],
                                               t_kvf, t_nrmk, t_kscr, t_ssK)
                            if isctx:
                                P.op("dve", lambda h: h.tensor_copy(out=krb[:, 256:288], in_=nrmk[:, 0:32]), rd=[t_nrmk], wr=[t_krb])
                            else:
                                rope(nrmk[:, 0:32], 1, 32, ropet[ti % 3][:, :],
                                     krb[:, 256:288].rearrange("p (g a b q) -> p g a b q", g=1, a=2, b=2),
                                     t1k, t2k, t_nrmk, t_rope[ti % 3], t_krb, t_t1k, t_t2k)
                            yield
                            for j in range(2):
                                P.op("pe", lambda h, j=j: h.transpose(tp2k[:, j, :], krb[:, j * 128:(j + 1) * 128], ident[:, :]),
                                     rd=[t_krb, t_const], wr=[t_tp2k], inc=False)
                            P.op("pe", lambda h: h.transpose(tp2k[0:32, 2, :], krb[:, 256:288], ident[:, :]),
                                 rd=[t_krb, t_const], wr=[t_tp2k])
                            yield
                            kdst = kstc if isctx else kst
                            c0 = (ti - NT) * 128 if isctx else ti * 128
                            P.op("dve", lambda h: h.tensor_copy(out=kdst[:, 0:2, c0:c0 + 128], in_=tp2k[:, 0:2, :]),
                                 rd=[t_tp2k], wr=[t_kst])
                            P.op("dve", lambda h: h.tensor_copy(out=kdst[0:32, 2, c0:c0 + 128], in_=tp2k[0:32, 2, :]),
                                 rd=[t_tp2k], wr=[t_kst])
                            yield

                    def backQ(ti):
                        isctx = ti >= NT
                        s = ti % 2
                        qf, t_qf = qf2[s], t_qf2[s]
                        if L == 0:
                            yield from gnorm_g(qf[:, :], 16, 64, R_QN0, nrm[:, :], bscr[:, :], ssQ[:, 0:16], ssQ[:, 16:32],
                                               t_qf, t_nrm, t_bscr, t_ssQ)
                            qv = qrb[:, :].rearrange("p (a i f d) -> p a f i d", a=2, i=4, f=2)
                            for a in range(2):
                                for f in range(2):
                                    hs = (8 * a + 4 * f) * 64
                                    if isctx:
                                        P.op("dve", lambda h, a=a, f=f, hs=hs: h.tensor_copy(
                                            out=qv[:, a, f, :, :], in_=nrm[:, hs:hs + 256].rearrange("p (i d) -> p i d", i=4)),
                                            rd=[t_nrm], wr=[t_qrb])
                                    else:
                                        o_ = (2 * a + f) * 256
                                        rope(nrm[:, hs:hs + 256], 4, 64, ropet[ti % 3][:, :],
                                             qv[:, a, f, :, :].rearrange("p i (x b q) -> p i x b q", x=2, b=2),
                                             t1b[:, o_:o_ + 256], t2b[:, o_:o_ + 256], t_nrm, t_rope[ti % 3], t_qrb, t_t1q[2 * a + f], t_t2q[2 * a + f])
                                    yield
                            for c in range(8):
                                P.op("pe", lambda h, c=c: h.transpose(tp2[:, c, :], qrb[:, c * 128:(c + 1) * 128], ident[:, :]),
                                     rd=[t_qrb, t_const], wr=[t_tp2], inc=(c == 7))
                            yield
                            P.op("act", lambda h: h.activation(out=QT[:, :, ti * 128:(ti + 1) * 128], in_=tp2[:, :, :], func=AF.Copy),
                                 rd=[t_tp2], wr=[t_QT])
                            yield
                        elif not isctx:
                            yield from gnorm_g(qf[:, 0:384], 1, 384, R_QAN, qrb[:, 0:384], bscr[:, 0:384], ssQ[:, 0:1], ssQ[:, 16:17],
                                               t_qf, t_qrb, t_bscr, t_ssQ)
                            for j in range(3):
                                P.op("pe", lambda h, j=j: h.transpose(tp2[:, j, :], qrb[:, j * 128:(j + 1) * 128], ident[:, :]),
                                     rd=[t_qrb, t_const], wr=[t_tp2], inc=(j == 2))
                            yield
                            P.op("dve", lambda h: h.tensor_copy(out=qanT[:, :, ti * 128:(ti + 1) * 128], in_=tp2[:, 0:3, :]),
                                 rd=[t_tp2], wr=[t_QT])
                            yield

                    pjam([front1(0)])
                    pjam([front2(0), front1(1)])
                    for ti in range(NPT):
                        gl = [backK(ti), backQ(ti)]
                        if ti + 1 < NPT:
                            gl.append(front2(ti + 1))
                        if ti + 2 < NPT:
                            gl.append(front1(ti + 2))
                        pjam(gl)
                    if L == 0:
                        for j in range(2):
                            P.dma("sp", payK0[j * 128:(j + 1) * 128, :], kst[:, j, :], rd=[t_kst], wr=[t_pay])
                            P.dma("sp", paycK0[j * 128:(j + 1) * 128, :], kstc[:, j, :], rd=[t_kst], wr=[t_pay])
                        P.dma("sp", payV0[:, :].rearrange("(t p) f -> p t f", p=128), vst[:, :, :], rd=[t_kst], wr=[t_pay])
                        P.dma("sp", paycV0[:, :].rearrange("(t p) f -> p t f", p=128), vstc[:, :, :], rd=[t_kst], wr=[t_pay])
                    else:
                        for j in range(2):
                            P.dma("sp", pay1[j * 128:(j + 1) * 128, :], kst[:, j, :], rd=[t_kst], wr=[t_pay])
                            P.dma("sp", payc1[j * 128:(j + 1) * 128, :], kstc[:, j, :], rd=[t_kst], wr=[t_pay])
                        P.dma("sp", pay1r[:, :], kst[0:32, 2, :], rd=[t_kst], wr=[t_pay])
                        P.dma("sp", payc1[256:288, :], kstc[0:32, 2, :], rd=[t_kst], wr=[t_pay])
                    P.flush()
                if stop == "P" and L == 1:
                    return nc
                t_gath = Trk()
                if L == 0:
                    P.allgather(payK0, payKG0, wr=[t_gath])
                    P.allgather(payV0, payVG0, wr=[t_gath])
                else:
                    P.allgather(pay1, payG1, wr=[t_gath])
                    P.allgather(pay1r, payG1r, wr=[t_gath])

                if stop == "G" and L == 1:
                    return nc
                wout_pre = None
                if L == 0:
                    wout_pre = sb(sl, "wout_pre", [128, NCH, D], BF16)
                    t_wo_pre = Trk()
                    for c in range(NCH):
                        P.dma("pool", wout_pre[:, c, :], w_out[L][c * 128:(c + 1) * 128, :], wr=[t_wo_pre])
                with ExitStack() as sa:
                    NH2 = 4 if L == 0 else 2
                    Vb = sb(sa, f"V{L}", [128, NKT, NH2, 65], BF16)
                    if L == 0:
                        KT = sb(sa, "KT", [128, 2, NKEY], BF16)
                    else:
                        KT = sb(sa, "KTg", [128, 2, NKEY], BF16)
                        kvnT = sb(sa, "kvnT", [128, 2, NKEY], BF16)
                        QTg = sb(sa, "QTg", [128, 2, QC], BF16)
                        wkvbp = [sb(sa, f"wkvbp{i}", [128, 2, 256], BF16) for i in range(2)]
                        wqbp = [sb(sa, f"wqbp{i}", [128, 3, 192], BF16) for i in range(2)]
                        t_wp = [Trk(), Trk()]
                        xnrm = sb(sa, "xnrm", [128, 2 * 384 + 6 * 256], F32)
                        xt1 = sb(sa, "xt1", [128, 2 * 384 + 6 * 256], F32)
                        xt2 = sb(sa, "xt2", [128, 2 * 384], F32)
                        xss = sb(sa, "xss", [128, 32], F32)
                        xrs = sb(sa, "xrs", [128, 32], F32)
                        rope1q = sb(sa, "rope1q", [128, NT, 64], F32)
                        t_kvn = Trk()
                        t_kvnc = [Trk() for _ in range(5)]
                    vstg = sb(sa, "vstg", [128, NT, 256], BF16) if L == 0 else None
                    NPB = 3
                    pt = [sb(sa, f"pt{i}", [128, 2, 512], BF16) for i in range(NPB)]
                    osb = [sb(sa, f"osb{i}", [66, 512], F32) for i in range(2)]
                    rinv = sb(sa, "rinv", [128, 8], F32)
                    st = [ps(sa, f"st{i}", [128, 2, 512], F32) for i in range(2)]
                    ot = [ps(sa, f"ot{i}", [128, 512], F32) for i in range(2)]
                    tpo = ps(sa, "tpo", [128, 4, 128], F32)
                    tpk = ps(sa, "tpk", [128, 4, 128], F32)
                    t_tpk = Trk()
                    t_V, t_KT, t_vstg = Trk(), Trk(), Trk()
                    t_KTc = [Trk() for _ in range(5)]
                    t_Vc = [Trk() for _ in range(5)]
                    chunk = lambda kt: 0 if kt < 2 else 1 + (kt - 2) // NT
                    t_st, t_pt = [[Trk(), Trk()], [Trk(), Trk()]], [Trk() for _ in range(NPB)]
                    t_ot, t_osb, t_tpo, t_rinv = [Trk(), Trk()], [Trk(), Trk()], Trk(), Trk()
                    P.op("pool", lambda h: h.memset(Vb[:, :, :, 64:65], 1.0), wr=[t_V] + t_Vc)
                    if L == 0:
                        for j in range(2):
                            P.dma("sp", KT[:, j, 0:CTX], paycK0[j * 128:(j + 1) * 128, :], rd=[t_gath], wr=[t_KTc[0]])
                        for r in range(4):
                            for j in range(2):
                                P.dma("sp", KT[:, j, CTX + r * TOK:CTX + (r + 1) * TOK],
                                      payKG0[r * 256 + j * 128:r * 256 + (j + 1) * 128, :], rd=[t_gath], wr=[t_KTc[1 + r]])
                        for r in range(-1, 4):
                            if r < 0:
                                P.dma("sp", vstg[:, 0:2, :], paycV0[:, :].rearrange("(t p) f -> p t f", p=128), rd=[t_gath], wr=[t_vstg])
                                n_, k0 = 2, 0
                            else:
                                P.dma("sp", vstg[:, :, :], payVG0[r * TOK:(r + 1) * TOK, :].rearrange("(t p) f -> p t f", p=128),
                                      rd=[t_gath], wr=[t_vstg])
                                n_, k0 = NT, 2 + r * NT
                            P.op("dve", lambda h, n_=n_, k0=k0: h.tensor_copy(
                                out=Vb[:, k0:k0 + n_, :, 0:64], in_=vstg[:, 0:n_, :].rearrange("p t (h d) -> p t h d", h=4)),
                                rd=[t_vstg], wr=[t_Vc[r + 1]])
                    else:
                        t_w = Trk()
                        for c in range(2):
                            P.dma("sp", kvnT[:, c, 0:CTX], payc1[c * 128:(c + 1) * 128, :], rd=[t_gath], wr=[t_kvnc[0]])
                        for r in range(4):
                            for c in range(2):
                                P.dma("sp", kvnT[:, c, CTX + r * TOK:CTX + (r + 1) * TOK],
                                      payG1[r * 256 + c * 128:r * 256 + (c + 1) * 128, :], rd=[t_gath], wr=[t_kvnc[1 + r]])
                        P.dma("sp", rope1q[:, :, :], rope1[:, :, :], wr=[t_w])
                        for hl in range(2):
                            P.dma("sp", KT[64:96, hl, 0:CTX], payc1[256:288, :], rd=[t_gath], wr=[t_KT])
                            for r in range(4):
                                P.dma("sp", KT[64:96, hl, CTX + r * TOK:CTX + (r + 1) * TOK],
                                      payG1r[r * 32:(r + 1) * 32, :], rd=[t_gath], wr=[t_KT])

                    if PRE and L == 0:
                        t_pre = Trk()
                        for dst_, src_, rows, cols in ((wmod1_bf, w_mod[1], D, 3 * D), (win1_bf, w_in[1], D, 1696),
                                                       (wkvb_bf, w_kvb, 256, 2048), (wqb_bf, w_qb, 384, 1536),
                                                       (wout1_bf, w_out[1], D, D)):
                            for r0 in range(0, rows, 128):
                                for cb in range(0, cols, 1024):
                                    ce = min(cols, cb + 1024)
                                    P.dma("pool", dst_[r0:r0 + 128, cb:ce], src_[r0:r0 + 128, cb:ce], rd=[t_KTc[4], t_Vc[4]], wr=[t_pre])
                    if stop == "AL" and L == 1:
                        P.flush()
                        return nc

                    def attend(kA, kB, qA, qB, vA, vB, kts, qlen, scale, fin):
                        n = len(kts)

                        def qk(i):
                            b = i % 2
                            P.op("pe", lambda h: h.matmul(st[b][:, 0, 0:qlen], kA(kts[i]), qA, start=True, stop=True),
                                 rd=[t_KT, t_KTc[chunk(kts[i])], t_QT], wr=[t_st[b][0]], inc=False)
                            P.op("pe", lambda h: h.matmul(st[b][:, 1, 0:qlen], kB(kts[i]), qB, start=True, stop=True),
                                 rd=[t_KT, t_KTc[chunk(kts[i])], t_QT], wr=[t_st[b][1]])

                        def ex(i):
                            b, s_ = i % 2, i % NPB
                            P.op("act", lambda h: h.activation(out=pt[s_][:, :, 0:qlen], in_=st[b][:, :, 0:qlen],
                                                               func=AF.Exp, scale=scale),
                                 rd=t_st[b], wr=[t_pt[s_]])

                        def pv(i):
                            s_ = i % NPB
                            P.op("pe", lambda h: h.matmul(ot[0][0:65, 0:qlen], vA(kts[i]), pt[s_][:, 0, 0:qlen],
                                                          start=(i == 0), stop=(i == n - 1)),
                                 rd=[t_V, t_Vc[chunk(kts[i])], t_pt[s_]], wr=[t_ot[0]], inc=False)
                            P.op("pe", lambda h: h.matmul(ot[1][0:65, 0:qlen], vB(kts[i]), pt[s_][:, 1, 0:qlen],
                                                          start=(i == 0), stop=(i == n - 1)),
                                 rd=[t_V, t_Vc[chunk(kts[i])], t_pt[s_]], wr=[t_ot[1]])
                        def prologue():
                            qk(0)
                            if n > 1:
                                qk(1)

                        def body():
                            for i in range(n):
                                ex(i)
                                if i + 2 < n:
                                    qk(i + 2)
                                pv(i)
                        nq = qlen // 128

                        def finalise():
                            _finalise(nq, qlen, fin)
                        return prologue, body, finalise

                    def _finalise(nq, qlen, fin):
                        for hh in range(2):
                            P.op("dve", lambda h, hh=hh: h.tensor_copy(out=osb[hh][0:65, 0:qlen], in_=ot[hh][0:65, 0:qlen]),
                                 rd=[t_ot[hh]], wr=[t_osb[hh]])
                        for hh in range(2):
                            for qi in range(nq):
                                P.op("pe", lambda h, hh=hh, qi=qi: h.transpose(
                                    tpo[:, qi, 0:66], osb[hh][:, qi * 128:(qi + 1) * 128], identf[0:66, 0:66]),
                                    rd=[t_osb[hh], t_const], wr=[t_tpo], inc=(qi == nq - 1))
                            P.op("dve", lambda h, hh=hh: h.reciprocal(out=rinv[:, hh * 4:hh * 4 + nq], in_=tpo[:, 0:nq, 64]),
                                 rd=[t_tpo], wr=[t_rinv])
                            for qi in range(nq):
                                fin(hh, qi, tpo[:, qi, 0:64], rinv[:, hh * 4 + qi:hh * 4 + qi + 1])

                    def run_blocks(blocks):
                        for bi, (pro, body, fin_) in enumerate(blocks):
                            if bi == 0:
                                pro()
                            body()
                            if bi + 1 < len(blocks):
                                blocks[bi + 1][0]()
                            fin_()

                    def mk_fin(tile0, colA, colB):
                        def fin(hh, qi, o_ap, r_ap):
                            ti = tile0 + qi
                            col = colA if hh == 0 else colB
                            P.op("dve", lambda h: h.scalar_tensor_tensor(
                                out=gsb[:, ti, col:col + 64], in0=o_ap, scalar=r_ap, in1=gsb[:, ti, col:col + 64],
                                op0=ALU.mult, op1=ALU.mult), rd=[t_tpo, t_rinv, t_g[ti]], wr=[t_g[ti]])
                        return fin

                    if L == 0:
                        sc0 = 64 ** -0.5
                        blocks0 = []
                        for a in range(2):
                            for i in range(4):
                                slot = 4 * a + i
                                hA, hB = 8 * a + i, 8 * a + 4 + i
                                kA = lambda kt, a=a: KT[0:64, a, kt * 128:(kt + 1) * 128]
                                kB = lambda kt, a=a: KT[64:128, a, kt * 128:(kt + 1) * 128]
                                vA = lambda kt, a=a: Vb[:, kt, 2 * a, :]
                                vB = lambda kt, a=a: Vb[:, kt, 2 * a + 1, :]
                                for qb in range(4):
                                    blocks0.append(attend(kA, kB, QT[0:64, slot, qb * 512:(qb + 1) * 512],
                                                          QT[64:128, slot, qb * 512:(qb + 1) * 512], vA, vB,
                                                          list(range(NKT)), 512, sc0, mk_fin(qb * 4, hA * 64, hB * 64)))
                                blocks0.append(attend(kA, kB, QT[0:64, slot, TOK:TOK + CTX], QT[64:128, slot, TOK:TOK + CTX], vA, vB,
                                                      [0, 1], 256, sc0, mk_fin(NT, hA * 64, hB * 64)))
                        run_blocks(blocks0)
                    else:
                        sc1 = 96 ** -0.5
                        t_x = {n_: Trk() for n_ in ("scr", "nrm", "t1", "t2", "kb", "ss")}

                        def rstd_pool(G, n):
                            P.op("act", lambda h: h.activation(out=xrs[:, 0:G], in_=xss[:, 0:G], func=AF.Sqrt,
                                                               bias=epsb[:, 0:1], scale=1.0 / n),
                                 rd=[t_x["ss"], t_const], wr=[t_x["ss"]])
                            P.op("dve", lambda h: h.reciprocal(out=xrs[:, 0:G], in_=xrs[:, 0:G]), rd=[t_x["ss"]], wr=[t_x["ss"]])

                        def jam(gens):
                            gens = list(gens)
                            while gens:
                                for g_ in list(gens):
                                    try:
                                        next(g_)
                                    except StopIteration:
                                        gens.remove(g_)

                        NBIG = 2
                        banks = [(st[0][:, 0, :], t_st[0][0]), (st[0][:, 1, :], t_st[0][1]), (st[1][:, 0, :], t_st[1][0]),
                                 (st[1][:, 1, :], t_st[1][1]), (ot[0][:, :], t_ot[0]), (ot[1][:, :], t_ot[1]),
                                 (tpo[:, :, :].rearrange("p g k -> p (g k)"), t_tpo), (tpk[:, :, :].rearrange("p g k -> p (g k)"), t_tpk)]

                        def mkset(s_):
                            big = s_ < NBIG
                            o = s_ * 384 if big else NBIG * 384 + (s_ - NBIG) * 256
                            w_ = 384 if big else 256
                            bk, t_bk = banks[s_]
                            T = {n_: Trk() for n_ in ("nrm", "t1", "t2", "ss")}
                            T["tk"] = t_bk
                            return dict(nrm=xnrm[:, o:o + w_], t1=xt1[:, o:o + w_], t2=(xt2[:, s_ * 384:(s_ + 1) * 384] if big else None),
                                        ss=xss[:, s_ * 4:s_ * 4 + 4], rs=xrs[:, s_ * 4:s_ * 4 + 4],
                                        tk=bk.rearrange("p (g k) -> p g k", g=4), s=s_, pb=bk, tb=t_bk, big=big, T=T)
                        sets = [mkset(i_) for i_ in range(8)]

                        def roll(factories):
                            free, active, pending = list(range(len(sets))), [], list(factories)
                            while pending or active:
                                for pi, (need_big, fn) in enumerate(pending):
                                    if need_big:
                                        cand = [i_ for i_ in free if sets[i_]["big"]]
                                    else:
                                        cand = [i_ for i_ in free if not sets[i_]["big"]]
                                        if not cand and not any(nb for nb, _ in pending):
                                            cand = list(free)
                                    if cand:
                                        free.remove(cand[0])
                                        active.append((fn(sets[cand[0]]), cand[0]))
                                        pending.pop(pi)
                                        break
                                for item in list(active):
                                    try:
                                        next(item[0])
                                    except StopIteration:
                                        active.remove(item)
                                        free.append(item[1])

                        def load_pair_w(p):
                            qn_, kvs_, qbs_ = ("sp", wkvb_bf, wqb_bf) if PRE else ("pool", w_kvb, w_qb)
                            for c in range(2):
                                P.dma(qn_, wkvbp[p % 2][:, c, :], kvs_[c * 128:(c + 1) * 128, p * 256:(p + 1) * 256], wr=[t_wp[p % 2]])
                            for c in range(3):
                                P.dma(qn_, wqbp[p % 2][:, c, :], qbs_[c * 128:(c + 1) * 128, p * 192:(p + 1) * 192], wr=[t_wp[p % 2]])

                        def rstd_(S_, G, n):
                            T = S_["T"]
                            P.op("act", lambda h: h.activation(out=S_["rs"][:, 0:G], in_=S_["ss"][:, 0:G], func=AF.Sqrt,
                                                               bias=epsb[:, 0:1], scale=1.0 / n), rd=[T["ss"], t_const], wr=[T["ss"]])
                            P.op("dve", lambda h: h.reciprocal(out=S_["rs"][:, 0:G], in_=S_["rs"][:, 0:G]), rd=[T["ss"]], wr=[T["ss"]])

                        wcomb = sb(sa, "wcomb", [128, 96], F32)
                        P.op("dve", lambda h: h.tensor_copy(out=wcomb[:, :], in_=rep_sb[:, R_QN1:R_QN1 + 96]), rd=[t_const], wr=[t_w])
                        P.op("dve", lambda h: h.tensor_tensor(out=wcomb[:, 0:64], in0=wcomb[:, 0:64], in1=rep_sb[:, R_KNN:R_KNN + 64],
                                                              op=ALU.mult), rd=[t_const, t_w], wr=[t_w])

                        def qprep(p, tb, S_):
                            T = S_["T"]
                            pb = S_["pb"][:, 0:384].rearrange("p (u x) -> p u x", u=2)
                            pb4 = S_["pb"][:, 0:384].rearrange("p (g n) -> p g n", g=4)
                            t_pb = S_["tb"]
                            for u in range(2):
                                ti = 2 * tb + u
                                for c in range(3):
                                    P.op("pe", lambda h, u=u, ti=ti, c=c: h.matmul(
                                        pb[:, u, :], qanT[:, c, ti * 128:(ti + 1) * 128], wqbp[p % 2][:, c, :],
                                        start=(c == 0), stop=(c == 2)), rd=[t_QT, t_wp[p % 2]], wr=[t_pb], inc=(c == 2))
                            yield
                            g4 = lambda t_: t_[:, 0:384].rearrange("p (g n) -> p g n", g=4)
                            P.op("act", lambda h: h.activation(out=S_["t1"], in_=S_["pb"][:, 0:384], func=AF.Square),
                                 rd=[t_pb], wr=[T["t1"]])
                            yield
                            P.op("dve", lambda h: h.tensor_reduce(out=S_["ss"], in_=g4(S_["t1"]), axis=AX.X, op=ALU.add),
                                 rd=[T["t1"]], wr=[T["ss"]])
                            yield
                            rstd_(S_, 4, 96)
                            yield
                            P.op("dve", lambda h: h.tensor_tensor(out=g4(S_["nrm"]), in0=pb4, in1=bc(S_["rs"], [128, 4, 96], 2),
                                                                  op=ALU.mult), rd=[t_pb, T["ss"]], wr=[T["nrm"]])
                            yield
                            P.op("dve", lambda h: h.tensor_tensor(out=g4(S_["nrm"]), in0=g4(S_["nrm"]),
                                                                  in1=bc(wcomb[:, :], [128, 4, 96], 1), op=ALU.mult),
                                 rd=[T["nrm"], t_w], wr=[T["nrm"]])
                            yield
                            for u in range(2):
                                ti = 2 * tb + u
                                d3 = S_["nrm"][:, u * 192:(u + 1) * 192].rearrange("p (g n) -> p g n", g=2)[:, :, 64:96]
                                rope(S_["nrm"][:, u * 192:(u + 1) * 192], 2, 96, rope1q[:, ti, :],
                                     d3.rearrange("p g (a b q) -> p g a b q", a=2, b=2),
                                     S_["t1"], S_["t2"], T["nrm"], t_w, T["nrm"], T["t1"], T["t2"], off=64, rn=32)
                                yield
                            for g in range(4):
                                P.op("pe", lambda h, g=g: h.transpose(S_["tk"][0:96, g, :], S_["nrm"][:, g * 96:(g + 1) * 96], identf[:, :]),
                                     rd=[T["nrm"], t_const], wr=[T["tk"]], inc=(g == 3))
                            yield
                            P.op("act", lambda h: h.activation(
                                out=QTg[0:96, :, tb * 256:(tb + 1) * 256].rearrange("p h (u k) -> p h u k", u=2),
                                in_=S_["tk"][0:96, 0:4, :].rearrange("p (u h) k -> p h u k", h=2), func=AF.Copy),
                                rd=[T["tk"]], wr=[t_QT])
                            yield

                        def expand(p, kt0, S_):
                            T = S_["T"]
                            e3 = S_["pb"].rearrange("p (u x) -> p u x", u=2)
                            t_pb = S_["tb"]
                            for u in range(2):
                                for c in range(2):
                                    P.op("pe", lambda h, u=u, c=c: h.matmul(
                                        e3[:, u, :], kvnT[:, c, (kt0 + u) * 128:(kt0 + u + 1) * 128],
                                        wkvbp[p % 2][:, c, :], start=(c == 0), stop=(c == 1)),
                                        rd=[t_kvn, t_kvnc[chunk(kt0)], t_wp[p % 2]], wr=[t_pb], inc=(c == 1))
                            yield
                            e5 = e3.rearrange("p u (h x d) -> p u h x d", h=2, x=2)
                            v4 = lambda t_: t_[:, 0:256].rearrange("p (u h d) -> p u h d", u=2, h=2)
                            v3 = lambda t_: t_[:, 0:256].rearrange("p (g d) -> p g d", d=64)
                            P.op("act", lambda h: h.activation(out=v4(S_["t1"]), in_=e5[:, :, :, 0, :], func=AF.Square),
                                 rd=[t_pb], wr=[T["t1"]])
                            P.op("act", lambda h: h.activation(out=Vb[:, kt0:kt0 + 2, :, 0:64], in_=e5[:, :, :, 1, :], func=AF.Copy),
                                 rd=[t_pb], wr=[t_V])
                            yield
                            P.op("dve", lambda h: h.tensor_reduce(out=S_["ss"], in_=v3(S_["t1"]), axis=AX.X, op=ALU.add),
                                 rd=[T["t1"]], wr=[T["ss"]])
                            yield
                            rstd_(S_, 4, 64)
                            yield
                            P.op("dve", lambda h: h.tensor_tensor(out=v4(S_["nrm"]), in0=e5[:, :, :, 0, :],
                                                                  in1=bc(S_["rs"].rearrange("p (u h) -> p u h", u=2), [128, 2, 2, 64], 3),
                                                                  op=ALU.mult), rd=[t_pb, T["ss"]], wr=[T["nrm"]])
                            yield
                            for g in range(4):
                                P.op("pe", lambda h, g=g: h.transpose(S_["tk"][0:64, g, :], S_["nrm"][:, g * 64:(g + 1) * 64], identf[:, :]),
                                     rd=[T["nrm"], t_const], wr=[T["tk"]], inc=(g == 3))
                            yield
                            P.op("act", lambda h: h.activation(
                                out=KT[0:64, :, kt0 * 128:(kt0 + 2) * 128].rearrange("p h (u k) -> p h u k", u=2),
                                in_=S_["tk"][0:64, 0:4, :].rearrange("p (u h) k -> p h u k", h=2), func=AF.Copy),
                                rd=[T["tk"]], wr=[t_KT])
                            yield

                        load_pair_w(0)
                        for p in range(8):
                            if p + 1 < 8:
                                load_pair_w(p + 1)
                            roll([(True, (lambda S_, tb=tb, p=p: qprep(p, tb, S_))) for tb in range(NT // 2)] +
                                 [(False, (lambda S_, kt0=kt0, p=p: expand(p, kt0, S_))) for kt0 in range(0, NKT, 2)])
                            kA = lambda kt: KT[0:96, 0, kt * 128:(kt + 1) * 128]
                            kB = lambda kt: KT[0:96, 1, kt * 128:(kt + 1) * 128]
                            vA = lambda kt: Vb[:, kt, 0, :]
                            vB = lambda kt: Vb[:, kt, 1, :]
                            run_blocks([attend(kA, kB, QTg[0:96, 0, qb * 512:(qb + 1) * 512], QTg[0:96, 1, qb * 512:(qb + 1) * 512],
                                               vA, vB, list(range(NKT)), 512, sc1, mk_fin(qb * 4, (2 * p) * 64, (2 * p + 1) * 64))
                                        for qb in range(4)])
                    P.flush()
                with ExitStack() as se:
                    wout = wout_pre if wout_pre is not None else sb(se, f"wout{L}", [128, NCH, D], BF16)
                    ogT = [sb(se, f"ogT{i}", [128, NCH, 128], BF16) for i in range(2)]
                    xs2 = [sb(se, f"xs2{i}", [128, D], F32) for i in range(2)]
                    xo = [sb(se, f"xo{i}", [128, D], F32) for i in range(2)]
                    tpe = [ps(se, f"tpe{i}", [128, NCH, 128], BF16) for i in range(2)]
                    ps_y = [ps(se, f"ps_y{i}", [128, D], F32) for i in range(2)]
                    t_wo = Trk()
                    t_og, t_xs2, t_xo, t_py, t_tpe = ([Trk(), Trk()] for _ in range(5))
                    if wout_pre is None:
                        for c in range(NCH):
                            if L == 1 and PRE:
                                P.dma("sp", wout[:, c, :], wout1_bf[c * 128:(c + 1) * 128, :], wr=[t_wo])
                            else:
                                P.dma("pool", wout[:, c, :], w_out[L][c * 128:(c + 1) * 128, :], wr=[t_wo])
                    ada_next = None
                    if L == 0 and 1 in layers:
                        ld_, ada_next = adaln(1, se)
                        ld_()

                    def etile(ti):
                        isctx = ti >= NT
                        cond = 1 if isctx else 0
                        s = ti % 2
                        if L == 0:
                            src_ = ctx_b[(ti - NT) * 128:(ti - NT + 1) * 128, :] if isctx else x_own[ti * 128:(ti + 1) * 128, :]
                            dst_ = ctx1s[(ti - NT) * 128:(ti - NT + 1) * 128, :] if isctx else x1s[ti * 128:(ti + 1) * 128, :]
                        else:
                            src_ = x1s[ti * 128:(ti + 1) * 128, :]
                            dst_ = out_d[ti * 128:(ti + 1) * 128, :]
                        P.dma("sp", xs2[s][:, :], src_, wr=[t_xs2[s]])
                        for c in range(NCH):
                            P.op("pe", lambda h, c=c: h.transpose(tpe[s][:, c, :], gsb[:, ti, c * 128:(c + 1) * 128], ident[:, :]),
                                 rd=[t_g[ti], t_const], wr=[t_tpe[s]], inc=(c == NCH - 1))
                        yield
                        P.op("act", lambda h: h.activation(out=ogT[s][:, :, :], in_=tpe[s][:, :, :], func=AF.Copy),
                             rd=[t_tpe[s]], wr=[t_og[s]])
                        yield
                        for half in range(2):
                            for c in range(NCH):
                                P.op("pe", lambda h, c=c, half=half: h.matmul(
                                    ps_y[s][:, half * 512:(half + 1) * 512], ogT[s][:, c, :],
                                    wout[:, c, half * 512:(half + 1) * 512], start=(c == 0), stop=(c == NCH - 1)),
                                    rd=[t_og[s], t_wo], wr=[t_py[s]], inc=(c == NCH - 1))
                        yield
                        P.op("dve", lambda h: h.tensor_tensor(out=xo[s][:, :], in0=ps_y[s][:, :], in1=gate[L][:, cond, :], op=ALU.mult),
                             rd=[t_py[s], t_const], wr=[t_xo[s]])
                        yield
                        P.op("pool", lambda h: h.tensor_tensor(out=xo[s][:, :], in0=xo[s][:, :], in1=xs2[s][:, :], op=ALU.add),
                             rd=[t_xo[s], t_xs2[s]], wr=[t_xo[s]])
                        yield
                        P.dma("sp", dst_, xo[s][:, :], rd=[t_xo[s]])
                        yield

                    def ejam(gens):
                        gens = list(gens)
                        while gens:
                            for g_ in list(gens):
                                try:
                                    next(g_)
                                except StopIteration:
                                    gens.remove(g_)
                    for ti in range(0, NQT, 2):
                        ejam([etile(ti), etile(ti + 1)])
                    if ada_next is not None:
                        ada_next()
                    P.flush()
    return nc


def _rope_table(rot_dim, tok):
    row = (tok // 64).astype(np.float32)
    col = (tok % 64).astype(np.float32)
    axis_dim = rot_dim // 2
    inv = np.power(np.float32(10000.0), -np.arange(0, axis_dim, 2, dtype=np.float32) / np.float32(axis_dim)).astype(np.float32)
    ar = row[:, None] * inv[None, :]
    ac = col[:, None] * inv[None, :]
    ang = np.concatenate([ar, ar, ac, ac], axis=-1).astype(np.float32)
    q = rot_dim // 4
    sign = np.concatenate([-np.ones(q), np.ones(q), -np.ones(q), np.ones(q)]).astype(np.float32)
    return np.concatenate([np.cos(ang), np.sin(ang) * sign[None, :]], axis=-1).astype(np.float32)


def _colT(v, n):
    return np.ascontiguousarray(np.asarray(v, np.float32).reshape(n, 128).T)


def make_in_maps(inp):
    f = lambda k: np.ascontiguousarray(np.asarray(inp[k], np.float32))
    repv = np.concatenate([f("l0_b_mod")[2048:], f("l1_b_mod")[2048:], f("l0_q_norm"), f("l0_k_norm"),
                           f("l1_kv_a_norm"), f("l1_k_rope_norm"), f("l1_q_a_norm"), f("l1_q_norm"),
                           f("l1_k_nope_norm")]).astype(np.float32)
    assert repv.shape[0] == NREP
    rep = np.ascontiguousarray(np.broadcast_to(repv[None, :], (128, NREP)))
    ident = np.eye(128, dtype=np.float32)
    shared = {"rep": rep, "ident": ident.astype(ml_dtypes.bfloat16), "identf": ident,
              "l0_w_mod": f("l0_w_mod"), "l1_w_mod": f("l1_w_mod"), "l0_w_in": f("l0_w_in"), "l1_w_in": f("l1_w_in"),
              "l0_w_out": f("l0_w_out"), "l1_w_out": f("l1_w_out"), "l1_w_kv_b": f("l1_w_kv_b"), "l1_w_q_b": f("l1_w_q_b")}
    x, c, ctx, c_ctx = f("x"), f("c"), f("ctx"), f("c_ctx")
    maps = []
    for core in range(8):
        b, qt = core // 4, core % 4
        tok = qt * TOK + np.arange(TOK)
        m = dict(shared)
        m["x_own"] = np.ascontiguousarray(x[b, qt * TOK:(qt + 1) * TOK])
        m["ctx_b"] = np.ascontiguousarray(ctx[b])
        m["vecT"] = np.ascontiguousarray(np.concatenate(
            [_colT(c[b], 8), _colT(c_ctx, 8), _colT(f("l0_norm"), 8), _colT(f("l0_b_mod"), 24),
             _colT(f("l1_norm"), 8), _colT(f("l1_b_mod"), 24)], axis=1))
        for name, rd in (("rope0", 64), ("rope1", 32)):
            t = _rope_table(rd, tok)
            m[name] = np.ascontiguousarray(t.reshape(NT, 128, 2 * rd).transpose(1, 0, 2))
        maps.append(m)
    return maps


_NC_CACHE = {}


def kernel(**inputs):
    if "nc" not in _NC_CACHE:
        _NC_CACHE["nc"] = build()
    maps = make_in_maps(inputs)
    res = run_bass_kernel_spmd(_NC_CACHE["nc"], maps, core_ids=list(range(8)))
    out = np.empty((2, 4 * TOK, D), np.float32)
    for core in range(8):
        b, qt = core // 4, core % 4
        out[b, qt * TOK:(qt + 1) * TOK] = np.asarray(res.results[core]["out"], np.float32)
    return out
```

```python
from contextlib import ExitStack
import numpy as np
import ml_dtypes
import concourse.bass as bass
import concourse.mybir as mybir
from concourse.bass_utils import run_bass_kernel_spmd

F32 = mybir.dt.float32
BF16 = mybir.dt.bfloat16
AF = mybir.ActivationFunctionType
ALU = mybir.AluOpType
AX = mybir.AxisListType

D = 1024
NCH = 8
TOK = 2048
NT = 16
CTX = 256
NKEY = CTX + 4 * TOK
NKT = NKEY // 128
EPS = 1e-6
GROUPS = [[0, 1, 2, 3], [4, 5, 6, 7]]

R_G0, R_G1, R_QN0, R_KN0, R_KVA, R_KRN, R_QAN, R_QN1, R_KNN = 0, 1024, 2048, 2112, 2176, 2432, 2464, 2848, 2944
NREP = 3008
V_CB, V_CC, V_N0, V_B0, V_N1, V_B1 = 0, 8, 16, 24, 48, 56
NVEC = 80


class Trk:
    __slots__ = ("w", "r", "ep")

    def __init__(self):
        self.w = {}
        self.r = {}
        self.ep = -1


class Prog:
    ENG = ("pe", "act", "dve", "pool", "sp")

    def __init__(self, nc, es):
        self.nc = nc
        self.q = {e: [] for e in self.ENG}
        self.sem = {e: es.enter_context(nc.semaphore("s_" + e)) for e in ("pe", "act", "dve", "pool")}
        self.cnt = {e: 0 for e in self.sem}
        self.seen = {e: {} for e in self.ENG}
        self.semobj = dict(self.sem)
        self.epoch = 0
        K = 6
        self.ring = {}
        for qn in ("sp", "pool"):
            self.ring[qn] = []
            for i in range(K):
                sm = es.enter_context(nc.semaphore(f"d_{qn}{i}"))
                self.ring[qn].append(sm)
                self.semobj[f"d_{qn}{i}"] = sm
        self.ringn = {qn: 0 for qn in self.ring}
        self.ccsem = es.enter_context(nc.semaphore("s_cc"))
        self.semobj["cc"] = self.ccsem
        self.ccn = 0

    def _tr(self, t):
        if t.ep != self.epoch:
            t.w = {}
            t.r = {}
            t.ep = self.epoch
        return t

    def _waits(self, eng, rd, wr):
        need = {}
        for t in rd:
            t = self._tr(t)
            for k, v in t.w.items():
                if k == eng and eng == "pe":
                    continue
                if need.get(k, 0) < v:
                    need[k] = v
        for t in wr:
            t = self._tr(t)
            for dct in (t.w, t.r):
                for k, v in dct.items():
                    if k == eng:
                        continue
                    if need.get(k, 0) < v:
                        need[k] = v
        out = []
        sn = self.seen[eng]
        for k, v in need.items():
            if sn.get(k, 0) < v:
                sn[k] = v
                out.append((self.semobj[k], v))
        return out

    def _mark(self, key, val, rd, wr):
        for t in rd:
            t = self._tr(t)
            if t.r.get(key, 0) < val:
                t.r[key] = val
        for t in wr:
            t = self._tr(t)
            if t.w.get(key, 0) < val:
                t.w[key] = val

    def op(self, eng, fn, rd=(), wr=(), inc=True):
        waits = self._waits(eng, rd, wr)
        if inc:
            self.cnt[eng] += 1
            val = self.cnt[eng]
        else:
            val = self.cnt[eng] + 1
        sem = self.sem[eng]

        def emit(h):
            for (sm, v) in waits:
                h.wait_ge(sm, v)
            ins = fn(h)
            if inc:
                ins.then_inc(sem, 1)
        self.q[eng].append(emit)
        self._mark(eng, val, rd, wr)

    def dma(self, qn, out, in_, rd=(), wr=(), **kw):
        waits = self._waits(qn, rd, wr)
        n = self.ringn[qn]
        self.ringn[qn] += 1
        K = len(self.ring[qn])
        slot, gen = n % K, n // K
        sem = self.ring[qn][slot]
        key = f"d_{qn}{slot}"
        prev, val = 16 * gen, 16 * (gen + 1)
        need_prev = gen > 0 and self.seen[qn].get(key, 0) < prev
        if need_prev:
            self.seen[qn][key] = prev

        def emit(h):
            for (sm, v) in waits:
                h.wait_ge(sm, v)
            if need_prev:
                h.wait_ge(sem, prev)
            h.dma_start(out=out, in_=in_, **kw).then_inc(sem, 16)
        self.q[qn].append(emit)
        self._mark(key, val, rd, wr)

    def allgather(self, src, dst, rd=(), wr=()):
        waits = self._waits("pool", rd, wr)
        self.ccn += 1
        val = self.ccn
        sem = self.ccsem

        def emit(h):
            for (sm, v) in waits:
                h.wait_ge(sm, v)
            h.collective_compute("AllGather", ALU.bypass, replica_groups=GROUPS,
                                 ins=[src.ap().opt()], outs=[dst.ap().opt()]).then_inc(sem)
        self.q["pool"].append(emit)
        self._mark("cc", val, rd, wr)

    def flush(self, final_waits=()):
        for qn in self.ring:
            n = self.ringn[qn]
            K = len(self.ring[qn])
            for slot in range(K):
                cntslot = (n - slot + K - 1) // K
                if cntslot > 0 and self.seen[qn].get(f"d_{qn}{slot}", 0) < 16 * cntslot:
                    self.seen[qn][f"d_{qn}{slot}"] = 16 * cntslot
                    sem = self.ring[qn][slot]
                    self.q[qn].append(lambda h, sem=sem, v=16 * cntslot: h.wait_ge(sem, v))
        if self.ccn > 0 and self.seen["pool"].get("cc", 0) < self.ccn:
            self.seen["pool"]["cc"] = self.ccn
            self.q["pool"].append(lambda h, v=self.ccn: h.wait_ge(self.ccsem, v))
        q = self.q
        self.q = {e: [] for e in self.ENG}
        with self.nc.Block() as block:
            @block.tensor
            def _(e):
                for f in q["pe"]:
                    f(e)

            @block.scalar
            def _(e):
                for f in q["act"]:
                    f(e)

            @block.vector
            def _(e):
                for f in q["dve"]:
                    f(e)

            @block.gpsimd
            def _(e):
                for f in q["pool"]:
                    f(e)

            @block.sync
            def _(e):
                for f in q["sp"]:
                    f(e)
        self.epoch += 1


def bc(ap, shape, axis):
    return ap.unsqueeze(axis).to_broadcast(list(shape))


def build(layers=(0, 1), dbg=False, stop=None):
    nc = bass.Bass("TRN2", target_bir_lowering=False)
    dt_in = lambda name, shape, dt=F32: nc.dram_tensor(name, list(shape), dt, kind="ExternalInput")
    x_own = dt_in("x_own", [TOK, D])
    ctx_b = dt_in("ctx_b", [CTX, D])
    vecT = dt_in("vecT", [128, NVEC])
    rep = dt_in("rep", [128, NREP])
    rope0 = dt_in("rope0", [128, NT, 128])
    rope1 = dt_in("rope1", [128, NT, 64])
    ident_d = dt_in("ident", [128, 128], BF16)
    identf_d = dt_in("identf", [128, 128])
    w_mod = [dt_in("l0_w_mod", [D, 3 * D]), dt_in("l1_w_mod", [D, 3 * D])]
    w_in = [dt_in("l0_w_in", [D, 2560]), dt_in("l1_w_in", [D, 1696])]
    w_out = [dt_in("l0_w_out", [D, D]), dt_in("l1_w_out", [D, D])]
    w_kvb = dt_in("l1_w_kv_b", [256, 2048])
    w_qb = dt_in("l1_w_q_b", [384, 1536])
    out_d = nc.dram_tensor("out", [TOK, D], F32, kind="ExternalOutput")
    okind = dict(kind="ExternalOutput") if (dbg or 1 not in layers) else {}
    if 0 not in layers:
        okind = dict(kind="ExternalInput")
    x1s = nc.dram_tensor("x1s", [TOK, D], F32, **okind)
    ctx1s = nc.dram_tensor("ctx1s", [CTX, D], F32, **okind)
    payK0 = nc.dram_tensor("payK0", [256, TOK], BF16)
    payKG0 = nc.dram_tensor("payKG0", [1024, TOK], BF16)
    payV0 = nc.dram_tensor("payV0", [TOK, 256], BF16)
    payVG0 = nc.dram_tensor("payVG0", [4 * TOK, 256], BF16)
    paycK0 = nc.dram_tensor("paycK0", [256, CTX], BF16)
    paycV0 = nc.dram_tensor("paycV0", [CTX, 256], BF16)
    pay1 = nc.dram_tensor("pay1", [256, TOK], BF16)
    payG1 = nc.dram_tensor("payG1", [4 * 256, TOK], BF16)
    pay1r = nc.dram_tensor("pay1r", [32, TOK], BF16)
    payG1r = nc.dram_tensor("payG1r", [4 * 32, TOK], BF16)
    payc1 = nc.dram_tensor("payc1", [288, CTX], BF16)
    PRE = (0 in layers) and (1 in layers)
    wmod1_bf = nc.dram_tensor("wmod1_bf", [D, 3 * D], BF16)
    win1_bf = nc.dram_tensor("win1_bf", [D, 1696], BF16)
    wout1_bf = nc.dram_tensor("wout1_bf", [D, D], BF16)
    wkvb_bf = nc.dram_tensor("wkvb_bf", [256, 2048], BF16)
    wqb_bf = nc.dram_tensor("wqb_bf", [384, 1536], BF16)

    _names = {}

    def _un(n):
        _names[n] = _names.get(n, 0) + 1
        return f"{n}_{_names[n]}"

    es = ExitStack()
    with es:
        P = Prog(nc, es)

        def sb(stack, name, shape, dt):
            return stack.enter_context(nc.sbuf_tensor(_un("s_" + name), list(shape), dt))

        def ps(stack, name, shape, dt):
            return stack.enter_context(nc.psum_tensor(_un("p_" + name), list(shape), dt))

        vec_sb = sb(es, "vec_sb", [128, NVEC], F32)
        rep_sb = sb(es, "rep_sb", [128, NREP], F32)
        ident = sb(es, "ident", [128, 128], BF16)
        identf = sb(es, "identf", [128, 128], F32)
        modT = [sb(es, f"modT{l}", [128, 24, 2], F32) for l in range(2)]
        Amod = [sb(es, f"Amod{l}", [128, 8, 2], F32) for l in range(2)]
        gate = [sb(es, "gate0", [128, 2, D], F32), sb(es, "gate1", [128, 1, D], F32)]
        t_const = Trk()

        scT = sb(es, "scT", [128, NCH, 2], BF16)
        t_sc = Trk()

        def load_cast(dst_fn, src_d, rows, cols, t_dst, stage, t_stage, piece=1024):
            k = 0
            for r0 in range(0, rows, 128):
                for cb in range(0, cols, piece):
                    ce = min(cols, cb + piece)
                    dst = dst_fn(r0 // 128, cb, ce)
                    if k % 2 == 0 or not stage:
                        P.dma("pool", dst, src_d[r0:r0 + 128, cb:ce], wr=[t_dst])
                    else:
                        si = (k // 2) % len(stage)
                        sg = stage[si][:, 0:ce - cb]
                        P.dma("sp", sg, src_d[r0:r0 + 128, cb:ce], wr=[t_stage[si]])
                        if (k // 2) % 2 == 0:
                            P.op("act", lambda h, dst=dst, sg=sg: h.activation(out=dst, in_=sg, func=AF.Copy),
                                 rd=[t_stage[si]], wr=[t_dst])
                        else:
                            P.op("dve", lambda h, dst=dst, sg=sg: h.tensor_copy(out=dst, in_=sg), rd=[t_stage[si]], wr=[t_dst])
                    k += 1

        def adaln(l, stk):
            wm = sb(stk, "wm", [128, NCH, 3 * D], BF16)
            screp = sb(stk, "screp", [128, NCH, 2, 128], BF16)
            tmpm = sb(stk, "tmpm", [128, 8, 2], F32)
            modps = ps(stk, "modps", [128, 24, 2], F32)
            gps0 = ps(stk, "gps0", [128, 512], F32)
            gps = [gps0, gps0]
            t_wm, t_mod, t_tmp, t_ml = Trk(), Trk(), Trk(), Trk()
            t_g0 = Trk()
            t_gps = [t_g0, t_g0]

            def load():
                if l == 1 and PRE:
                    for c in range(NCH):
                        P.dma("sp", wm[:, c, :], wmod1_bf[c * 128:(c + 1) * 128, :], wr=[t_wm])
                else:
                    stg = [sb(stk, f"wstg{i}", [128, 1024], F32) for i in range(3)]
                    load_cast(lambda c, cb, ce: wm[:, c, cb:ce], w_mod[l], D, 3 * D, t_wm, stg, [Trk() for _ in stg])

            def compute():
                P.op("dve", lambda h: h.tensor_copy(out=screp[:, :, :, :], in_=bc(scT[:, :, :], [128, NCH, 2, 128], 3)),
                     rd=[t_sc], wr=[t_wm])
                for j in range(24):
                    for c in range(NCH):
                        P.op("pe", lambda h, j=j, c=c: h.matmul(modps[:, j, :], wm[:, c, j * 128:(j + 1) * 128],
                                                                 scT[:, c, :], start=(c == 0), stop=(c == NCH - 1)),
                             rd=[t_wm, t_sc], wr=[t_mod], inc=(c == NCH - 1))
                vb = V_B0 if l == 0 else V_B1
                vn = V_N0 if l == 0 else V_N1
                P.op("dve", lambda h: h.tensor_tensor(out=modT[l][:, :, :], in0=modps[:, :, :],
                                                      in1=bc(vec_sb[:, vb:vb + 24], [128, 24, 2], 2), op=ALU.add),
                     rd=[t_mod, t_const], wr=[t_ml, t_const])
                P.op("dve", lambda h: h.tensor_scalar(out=tmpm[:, :, :], in0=modT[l][:, 8:16, :], scalar1=1.0,
                                                      scalar2=None, op0=ALU.add), rd=[t_ml], wr=[t_tmp])
                P.op("dve", lambda h: h.tensor_tensor(out=Amod[l][:, :, :], in0=tmpm[:, :, :],
                                                      in1=bc(vec_sb[:, vn:vn + 8], [128, 8, 2], 2), op=ALU.mult),
                     rd=[t_tmp, t_const], wr=[t_const])
                for t in range(2 if l == 0 else 1):
                    for half in range(2):
                        g = gps[half]
                        for c in range(NCH):
                            P.op("pe", lambda h, g=g, c=c, t=t, half=half: h.matmul(
                                g[:, :], screp[:, c, t, :], wm[:, c, 2 * D + half * 512:2 * D + (half + 1) * 512],
                                start=(c == 0), stop=(c == NCH - 1)),
                                rd=[t_wm, t_sc], wr=[t_gps[half]], inc=(c == NCH - 1))
                        ro = R_G0 if l == 0 else R_G1
                        P.op("dve", lambda h, g=g, t=t, half=half, ro=ro: h.tensor_tensor(
                            out=gate[l][:, t, half * 512:(half + 1) * 512], in0=g[:, :],
                            in1=rep_sb[:, ro + half * 512:ro + (half + 1) * 512], op=ALU.add),
                            rd=[t_gps[half], t_const], wr=[t_const])
            return load, compute

        with ExitStack() as s0:
            P.dma("sp", vec_sb[:, :], vecT[:, :], wr=[t_const])
            P.dma("sp", rep_sb[:, :], rep[:, :], wr=[t_const])
            P.dma("sp", ident[:, :], ident_d[:, :], wr=[t_const])
            P.dma("sp", identf[:, :], identf_d[:, :], wr=[t_const])
            P.op("act", lambda h: h.activation(out=scT[:, :, :].rearrange("p c t -> p t c"),
                                               in_=vec_sb[:, V_CB:V_CB + 16].rearrange("p (t c) -> p t c", t=2),
                                               func=AF.Silu), rd=[t_const], wr=[t_sc])
            first = layers[0]
            ld_, cp_ = adaln(first, s0)
            ld_()
            cp_()
            P.flush()

        def gnorm(src, G, n, wcol, dst, scr, ssb, rsb, t_src, t_dst, t_scr, t_ss):
            s3 = src.rearrange("p (g n) -> p g n", g=G)
            d3 = dst.rearrange("p (g n) -> p g n", g=G)
            P.op("act", lambda h: h.activation(out=scr, in_=src, func=AF.Square), rd=[t_src], wr=[t_scr])
            P.op("dve", lambda h: h.tensor_reduce(out=ssb, in_=scr.rearrange("p (g n) -> p g n", g=G),
                                                  axis=AX.X, op=ALU.add), rd=[t_scr], wr=[t_ss])
            P.op("act", lambda h: h.activation(out=rsb, in_=ssb, func=AF.Sqrt, bias=epsb[:, 0:1], scale=1.0 / n),
                 rd=[t_ss, t_const], wr=[t_ss])
            P.op("dve", lambda h: h.reciprocal(out=rsb, in_=rsb), rd=[t_ss], wr=[t_ss])
            P.op("dve", lambda h: h.tensor_tensor(out=scr.rearrange("p (g n) -> p g n", g=G), in0=s3,
                                                  in1=bc(rsb, [128, G, n], 2), op=ALU.mult),
                 rd=[t_src, t_ss], wr=[t_scr])
            P.op("dve", lambda h: h.tensor_tensor(out=d3, in0=scr.rearrange("p (g n) -> p g n", g=G),
                                                  in1=bc(rep_sb[:, wcol:wcol + n], [128, G, n], 1), op=ALU.mult),
                 rd=[t_scr, t_const], wr=[t_dst])

        def rope(src, G, n, tab, dst5, t1, t2, t_src, t_tab, t_dst, t_t1, t_t2, off=0, rn=None):
            rn = rn or n
            q = rn // 4
            s3 = src.rearrange("p (g n) -> p g n", g=G)[:, :, off:off + rn]
            s5 = s3.rearrange("p g (a b q) -> p g a b q", a=2, b=2)
            cosb = bc(tab[:, 0:rn], [128, G, rn], 1)
            sin4 = tab[:, rn:2 * rn].rearrange("p (a b q) -> p a b q", a=2, b=2)
            a1 = t1[:, 0:G * rn].rearrange("p (g n) -> p g n", g=G)
            a2 = t2[:, 0:G * rn].rearrange("p (g a b q) -> p g a b q", g=G, a=2, b=2)
            P.op("dve", lambda h: h.tensor_tensor(out=a1, in0=s3, in1=cosb, op=ALU.mult),
                 rd=[t_src, t_tab], wr=[t_t1])
            for b in range(2):
                P.op("pool", lambda h, b=b: h.tensor_tensor(out=a2[:, :, :, b, :], in0=s5[:, :, :, 1 - b, :],
                                                            in1=bc(sin4[:, :, b, :], [128, G, 2, q], 1), op=ALU.mult),
                     rd=[t_src, t_tab], wr=[t_t2])
            a15 = t1[:, 0:G * rn].rearrange("p (g a b q) -> p g a b q", g=G, a=2, b=2)
            for a in range(2):
                P.op("dve", lambda h, a=a: h.tensor_tensor(out=dst5[:, :, a, :, :], in0=a15[:, :, a, :, :],
                                                           in1=a2[:, :, a, :, :], op=ALU.add),
                     rd=[t_t1, t_t2], wr=[t_dst])

        epsb = sb(es, "epsb", [128, 1], F32)
        P.op("dve", lambda h: h.memset(epsb[:, :], EPS), wr=[t_const])

        for L in layers:
            NQT = NT + 2 if L == 0 else NT
            NPT = NT + 2
            QC = NQT * 128
            WIN = 2560 if L == 0 else 1696
            with ExitStack() as sl:
                gsb = sb(sl, f"gsb{L}", [128, NQT, D], BF16)
                t_g = [Trk() for _ in range(NQT)]
                if L == 0:
                    QT = sb(sl, "QT", [128, 8, QC], BF16)
                else:
                    qanT = sb(sl, "qanT", [128, 3, QC], BF16)
                t_QT = Trk()
                t_pay = Trk()
                with ExitStack() as sp_:
                    RW = 128 if L == 0 else 64
                    win = sb(sp_, f"win{L}", [128, NCH, WIN], BF16)
                    ropet = [sb(sp_, f"rope{L}_{i}", [128, RW], F32) for i in range(3)]
                    xs = [sb(sp_, f"xs{i}", [128, D], F32) for i in range(2)]
                    scrF = sb(sp_, "scrF", [128, D], F32)
                    bscr = sb(sp_, "bscr", [128, D], F32)
                    kscr = sb(sp_, "kscr", [128, 512], F32)
                    t1b = sb(sp_, "t1b", [128, D], F32)
                    t2b = sb(sp_, "t2b", [128, D], F32)
                    t1k = sb(sp_, "t1k", [128, 256], F32)
                    t2k = sb(sp_, "t2k", [128, 256], F32)
                    nrmk = sb(sp_, "nrmk", [128, 256], F32)
                    qf2 = [sb(sp_, f"qf{i}", [128, D], F32) for i in range(2)]
                    kvf2 = [sb(sp_, f"kvf{i}", [128, 512], F32) for i in range(2)]
                    nrm = bscr
                    xn = sb(sp_, "xn", [128, D], BF16)
                    hT = [sb(sp_, f"hT{i}", [128, NCH, 128], BF16) for i in range(2)]
                    ssF = sb(sp_, "ssF", [128, 2], F32)
                    ssK = sb(sp_, "ssK", [128, 8], F32)
                    ssQ = sb(sp_, "ssQ", [128, 32], F32)
                    qrb = sb(sp_, "qrb", [128, D], BF16)
                    krb = sb(sp_, "krb", [128, 512], BF16)
                    tp = ps(sp_, "tp", [128, NCH, 128], BF16)
                    tp2 = ps(sp_, "tp2", [128, NCH, 128], BF16)
                    tp2k = ps(sp_, "tp2k", [128, NCH, 128], BF16)
                    ps_kv = ps(sp_, "ps_kv", [128, 512], F32)
                    ps_q = ps(sp_, "ps_q", [128, D], F32)
                    ps_g = ps(sp_, "ps_g", [128, D], F32)
                    t_win = Trk()
                    t_rope, t_xs, t_hT = [Trk(), Trk(), Trk()], [Trk(), Trk()], [Trk(), Trk()]
                    t_scrF, t_ssF, t_xn, t_tp, t_pkv, t_pq, t_pg = (Trk() for _ in range(7))
                    t_qf2, t_kvf2 = [Trk(), Trk()], [Trk(), Trk()]
                    t_bscr, t_kscr, t_t1, t_t2, t_t1k, t_t2k, t_nrmk, t_ssK, t_ssQ = (Trk() for _ in range(9))
                    t_nrm = t_bscr
                    t_qrb, t_krb, t_tp2, t_tp2k = (Trk() for _ in range(4))
                    t_t1q, t_t2q = [Trk() for _ in range(4)], [Trk() for _ in range(4)]
                    if L == 0:
                        kst = sb(sp_, "kst", [128, 2, TOK], BF16)
                        vst = sb(sp_, "vst", [128, NT, 256], BF16)
                        kstc = sb(sp_, "kstc", [128, 2, CTX], BF16)
                        vstc = sb(sp_, "vstc", [128, 2, 256], BF16)
                    else:
                        kst = sb(sp_, "kst1", [128, 3, TOK], BF16)
                        kstc = sb(sp_, "kstc1", [128, 3, CTX], BF16)
                    t_kst = Trk()
                    for c in range(NCH):
                        if L == 1 and PRE:
                            P.dma("sp", win[:, c, :], win1_bf[c * 128:(c + 1) * 128, :], wr=[t_win])
                            continue
                        pass
                    if not (L == 1 and PRE):
                        load_cast(lambda c, cb, ce: win[:, c, cb:ce], w_in[L], D, WIN, t_win, [bscr, scrF], [t_bscr, t_scrF])
                    rope_d = rope0 if L == 0 else rope1

                    def pjam(gens):
                        gens = list(gens)
                        while gens:
                            for g_ in list(gens):
                                try:
                                    next(g_)
                                except StopIteration:
                                    gens.remove(g_)

                    def gnorm_g(src, G, n, wcol, dst, scr_, ss_, rs_, t_src, t_dst, t_scr_, t_ss_):
                        s3 = src.rearrange("p (g n) -> p g n", g=G)
                        d3 = dst.rearrange("p (g n) -> p g n", g=G)
                        c3 = scr_.rearrange("p (g n) -> p g n", g=G)
                        P.op("act", lambda h: h.activation(out=scr_, in_=src, func=AF.Square), rd=[t_src], wr=[t_scr_])
                        yield
                        P.op("dve", lambda h: h.tensor_reduce(out=ss_, in_=c3, axis=AX.X, op=ALU.add), rd=[t_scr_], wr=[t_ss_])
                        yield
                        P.op("act", lambda h: h.activation(out=rs_, in_=ss_, func=AF.Sqrt, bias=epsb[:, 0:1], scale=1.0 / n),
                             rd=[t_ss_, t_const], wr=[t_ss_])
                        yield
                        P.op("dve", lambda h: h.reciprocal(out=rs_, in_=rs_), rd=[t_ss_], wr=[t_ss_])
                        yield
                        big = G * n >= 1024
                        if big:
                            hg = G // 2
                            P.op("pool", lambda h: h.tensor_tensor(out=c3[:, 0:hg, :], in0=s3[:, 0:hg, :],
                                                                   in1=bc(rs_[:, 0:hg], [128, hg, n], 2), op=ALU.mult),
                                 rd=[t_src, t_ss_], wr=[t_scr_])
                            P.op("dve", lambda h: h.tensor_tensor(out=c3[:, hg:G, :], in0=s3[:, hg:G, :],
                                                                  in1=bc(rs_[:, hg:G], [128, G - hg, n], 2), op=ALU.mult),
                                 rd=[t_src, t_ss_], wr=[t_scr_])
                        else:
                            P.op("dve", lambda h: h.tensor_tensor(out=c3, in0=s3, in1=bc(rs_, [128, G, n], 2), op=ALU.mult),
                                 rd=[t_src, t_ss_], wr=[t_scr_])
                        yield
                        P.op("pool", lambda h: h.tensor_tensor(out=d3, in0=c3, in1=bc(rep_sb[:, wcol:wcol + n], [128, G, n], 1),
                                                               op=ALU.mult), rd=[t_scr_, t_const], wr=[t_dst])
                        yield

                    def front1(ti):
                        isctx = ti >= NT
                        cond = 1 if isctx else 0
                        s = ti % 2
                        if L == 0:
                            src = ctx_b[(ti - NT) * 128:(ti - NT + 1) * 128, :] if isctx else x_own[ti * 128:(ti + 1) * 128, :]
                        else:
                            src = ctx1s[(ti - NT) * 128:(ti - NT + 1) * 128, :] if isctx else x1s[ti * 128:(ti + 1) * 128, :]
                        P.dma("sp", xs[s][:, :], src, wr=[t_xs[s]])
                        if not isctx:
                            P.dma("sp", ropet[ti % 3][:, :], rope_d[:, ti, :], wr=[t_rope[ti % 3]])
                        yield
                        P.op("act", lambda h: h.activation(out=scrF[:, :], in_=xs[s][:, :], func=AF.Square),
                             rd=[t_xs[s]], wr=[t_scrF])
                        yield
                        P.op("dve", lambda h: h.tensor_reduce(out=ssF[:, 0:1], in_=scrF[:, :], axis=AX.X, op=ALU.add),
                             rd=[t_scrF], wr=[t_ssF])
                        yield
                        P.op("act", lambda h: h.activation(out=ssF[:, 1:2], in_=ssF[:, 0:1], func=AF.Sqrt,
                                                           bias=epsb[:, 0:1], scale=1.0 / D), rd=[t_ssF, t_const], wr=[t_ssF])
                        yield
                        P.op("dve", lambda h: h.reciprocal(out=ssF[:, 1:2], in_=ssF[:, 1:2]), rd=[t_ssF], wr=[t_ssF])
                        yield
                        P.op("act", lambda h: h.activation(out=xn[:, :], in_=xs[s][:, :], func=AF.Copy, scale=ssF[:, 1:2]),
                             rd=[t_xs[s], t_ssF], wr=[t_xn])
                        yield
                        for c in range(NCH):
                            P.op("pe", lambda h, c=c: h.transpose(tp[:, c, :], xn[:, c * 128:(c + 1) * 128], ident[:, :]),
                                 rd=[t_xn, t_const], wr=[t_tp], inc=(c == NCH - 1))
                        yield
                        for c in range(NCH):
                            P.op("dve", lambda h, c=c: h.tensor_scalar(
                                out=hT[s][:, c, :], in0=tp[:, c, :], scalar1=Amod[L][:, c, cond:cond + 1],
                                scalar2=modT[L][:, c, cond:cond + 1], op0=ALU.mult, op1=ALU.add),
                                rd=[t_tp, t_const], wr=[t_hT[s]])
                            if c % 2 == 1:
                                yield

                    def front2(ti):
                        isctx = ti >= NT
                        s = ti % 2
                        qf, kvf, t_qf, t_kvf = qf2[s], kvf2[s], t_qf2[s], t_kvf2[s]

                        def proj(dst, c0, c1, t_dst):
                            for c in range(NCH):
                                P.op("pe", lambda h, c=c: h.matmul(dst, hT[s][:, c, :], win[:, c, c0:c1],
                                                                    start=(c == 0), stop=(c == NCH - 1)),
                                     rd=[t_hT[s], t_win], wr=[t_dst], inc=(c == NCH - 1))
                        need_q = (L == 0) or not isctx
                        if L == 0:
                            proj(ps_g[:, 0:512], 1536, 2048, t_pg)
                            proj(ps_g[:, 512:1024], 2048, 2560, t_pg)
                            yield
                            P.op("act", lambda h: h.activation(out=gsb[:, ti, :], in_=ps_g[:, :], func=AF.Silu),
                                 rd=[t_pg], wr=[t_g[ti]])
                            proj(ps_q[:, 0:512], 512, 1024, t_pq)
                            proj(ps_q[:, 512:1024], 1024, 1536, t_pq)
                            yield
                            P.op("act", lambda h: h.activation(out=qf[:, :], in_=ps_q[:, :], func=AF.Copy), rd=[t_pq], wr=[t_qf])
                            proj(ps_kv[:, 0:512], 0, 512, t_pkv)
                            yield
                            P.op("act", lambda h: h.activation(out=kvf[:, :], in_=ps_kv[:, :], func=AF.Copy), rd=[t_pkv], wr=[t_kvf])
                        else:
                            if need_q:
                                proj(ps_g[:, 0:512], 672, 1184, t_pg)
                                proj(ps_g[:, 512:1024], 1184, 1696, t_pg)
                                yield
                                P.op("act", lambda h: h.activation(out=gsb[:, ti, :], in_=ps_g[:, :], func=AF.Silu),
                                     rd=[t_pg], wr=[t_g[ti]])
                                proj(ps_q[:, 0:384], 288, 672, t_pq)
                                yield
                                P.op("act", lambda h: h.activation(out=qf[:, 0:384], in_=ps_q[:, 0:384], func=AF.Copy),
                                     rd=[t_pq], wr=[t_qf])
                            proj(ps_kv[:, 0:288], 0, 288, t_pkv)
                            yield
                            P.op("act", lambda h: h.activation(out=kvf[:, 0:288], in_=ps_kv[:, 0:288], func=AF.Copy),
                                 rd=[t_pkv], wr=[t_kvf])
                        yield

                    def backK(ti):
                        isctx = ti >= NT
                        s = ti % 2
                        kvf, t_kvf = kvf2[s], t_kvf2[s]
                        if L == 0:
                            yield from gnorm_g(kvf[:, 0:256], 4, 64, R_KN0, nrmk[:, 0:256], kscr[:, 0:256], ssK[:, 0:4], ssK[:, 4:8],
                                               t_kvf, t_nrmk, t_kscr, t_ssK)
                            if isctx:
                                P.op("dve", lambda h: h.tensor_copy(out=krb[:, 0:256], in_=nrmk[:, 0:256]), rd=[t_nrmk], wr=[t_krb])
                            else:
                                rope(nrmk[:, 0:256], 4, 64, ropet[ti % 3][:, :],
                                     krb[:, 0:256].rearrange("p (g a b q) -> p g a b q", g=4, a=2, b=2),
                                     t1k, t2k, t_nrmk, t_rope[ti % 3], t_krb, t_t1k, t_t2k)
                            yield
                            for j in range(2):
                                P.op("pe", lambda h, j=j: h.transpose(tp2k[:, j, :], krb[:, j * 128:(j + 1) * 128], ident[:, :]),
                                     rd=[t_krb, t_const], wr=[t_tp2k], inc=(j == 1))
                            yield
                            if isctx:
                                kd = kstc[:, :, (ti - NT) * 128:(ti - NT + 1) * 128]
                                vd = vstc[:, ti - NT, :]
                            else:
                                kd = kst[:, :, ti * 128:(ti + 1) * 128]
                                vd = vst[:, ti, :]
                            P.op("dve", lambda h: h.tensor_copy(out=kd, in_=tp2k[:, 0:2, :]), rd=[t_tp2k], wr=[t_kst])
                            P.op("pool", lambda h: h.tensor_copy(out=vd, in_=kvf[:, 256:512]), rd=[t_kvf], wr=[t_kst])
                            yield
                        else:
                            yield from gnorm_g(kvf[:, 0:256], 1, 256, R_KVA, krb[:, 0:256], kscr[:, 0:256], ssK[:, 0:1], ssK[:, 4:5],
                                               t_kvf, t_krb, t_kscr, t_ssK)
                            yield from gnorm_g(kvf[:, 256:288], 1, 32, R_KRN, nrmk[:, 0:32], kscr[:, 256:288], ssK[:, 1:2], ssK[:, 5:6],
                                               t_kvf, t_nrmk, t_kscr, t_ssK)
                            if isctx:
                                P.op("dve", lambda h: h.tensor_copy(out=krb[:, 256:288], in_=nrmk[:, 0:32]), rd=[t_nrmk], wr=[t_krb])
                            else:
                                rope(nrmk[:, 0:32], 1, 32, ropet[ti % 3][:, :],
                                     krb[:, 256:288].rearrange("p (g a b q) -> p g a b q", g=1, a=2, b=2),
                                     t1k, t2k, t_nrmk, t_rope[ti % 3], t_krb, t_t1k, t_t2k)
                            yield
                            for j in range(2):
                                P.op("pe", lambda h, j=j: h.transpose(tp2k[:, j, :], krb[:, j * 128:(j + 1) * 128], ident[:, :]),
                                     rd=[t_krb, t_const], wr=[t_tp2k], inc=False)
                            P.op("pe", lambda h: h.transpose(tp2k[0:32, 2, :], krb[:, 256:288], ident[:, :]),
                                 rd=[t_krb, t_const], wr=[t_tp2k])
                            yield
                            kdst = kstc if isctx else kst
                            c0 = (ti - NT) * 128 if isctx else ti * 128
                            P.op("dve", lambda h: h.tensor_copy(out=kdst[:, 0:2, c0:c0 + 128], in_=tp2k[:, 0:2, :]),
                                 rd=[t_tp2k], wr=[t_kst])
                            P.op("dve", lambda h: h.tensor_copy(out=kdst[0:32, 2, c0:c0 + 128], in_=tp2k[0:32, 2, :]),
                                 rd=[t_tp2k], wr=[t_kst])
                            yield

                    def backQ(ti):
                        isctx = ti >= NT
                        s = ti % 2
                        qf, t_qf = qf2[s], t_qf2[s]
                        if L == 0:
                            yield from gnorm_g(qf[:, :], 16, 64, R_QN0, nrm[:, :], bscr[:, :], ssQ[:, 0:16], ssQ[:, 16:32],
                                               t_qf, t_nrm, t_bscr, t_ssQ)
                            qv = qrb[:, :].rearrange("p (a i f d) -> p a f i d", a=2, i=4, f=2)
                            for a in range(2):
                                for f in range(2):
                                    hs = (8 * a + 4 * f) * 64
                                    if isctx:
                                        P.op("dve", lambda h, a=a, f=f, hs=hs: h.tensor_copy(
                                            out=qv[:, a, f, :, :], in_=nrm[:, hs:hs + 256].rearrange("p (i d) -> p i d", i=4)),
                                            rd=[t_nrm], wr=[t_qrb])
                                    else:
                                        o_ = (2 * a + f) * 256
                                        rope(nrm[:, hs:hs + 256], 4, 64, ropet[ti % 3][:, :],
                                             qv[:, a, f, :, :].rearrange("p i (x b q) -> p i x b q", x=2, b=2),
                                             t1b[:, o_:o_ + 256], t2b[:, o_:o_ + 256], t_nrm, t_rope[ti % 3], t_qrb, t_t1q[2 * a + f], t_t2q[2 * a + f])
                                    yield
                            for c in range(8):
                                P.op("pe", lambda h, c=c: h.transpose(tp2[:, c, :], qrb[:, c * 128:(c + 1) * 128], ident[:, :]),
                                     rd=[t_qrb, t_const], wr=[t_tp2], inc=(c == 7))
                            yield
                            P.op("act", lambda h: h.activation(out=QT[:, :, ti * 128:(ti + 1) * 128], in_=tp2[:, :, :], func=AF.Copy),
                                 rd=[t_tp2], wr=[t_QT])
                            yield
                        elif not isctx:
                            yield from gnorm_g(qf[:, 0:384], 1, 384, R_QAN, qrb[:, 0:384], bscr[:, 0:384], ssQ[:, 0:1], ssQ[:, 16:17],
                                               t_qf, t_qrb, t_bscr, t_ssQ)
                            for j in range(3):
                                P.op("pe", lambda h, j=j: h.transpose(tp2[:, j, :], qrb[:, j * 128:(j + 1) * 128], ident[:, :]),
                                     rd=[t_qrb, t_const], wr=[t_tp2], inc=(j == 2))
                            yield
                            P.op("dve", lambda h: h.tensor_copy(out=qanT[:, :, ti * 128:(ti + 1) * 128], in_=tp2[:, 0:3, :]),
                                 rd=[t_tp2], wr=[t_QT])
                            yield

                    pjam([front1(0)])
                    pjam([front2(0), front1(1)])
                    for ti in range(NPT):
                        gl = [backK(ti), backQ(ti)]
                        if ti + 1 < NPT:
                            gl.append(front2(ti + 1))
                        if ti + 2 < NPT:
                            gl.append(front1(ti + 2))
                        pjam(gl)
                    if L == 0:
                        for j in range(2):
                            P.dma("sp", payK0[j * 128:(j + 1) * 128, :], kst[:, j, :], rd=[t_kst], wr=[t_pay])
                            P.dma("sp", paycK0[j * 128:(j + 1) * 128, :], kstc[:, j, :], rd=[t_kst], wr=[t_pay])
                        P.dma("sp", payV0[:, :].rearrange("(t p) f -> p t f", p=128), vst[:, :, :], rd=[t_kst], wr=[t_pay])
                        P.dma("sp", paycV0[:, :].rearrange("(t p) f -> p t f", p=128), vstc[:, :, :], rd=[t_kst], wr=[t_pay])
                    else:
                        for j in range(2):
                            P.dma("sp", pay1[j * 128:(j + 1) * 128, :], kst[:, j, :], rd=[t_kst], wr=[t_pay])
                            P.dma("sp", payc1[j * 128:(j + 1) * 128, :], kstc[:, j, :], rd=[t_kst], wr=[t_pay])
                        P.dma("sp", pay1r[:, :], kst[0:32, 2, :], rd=[t_kst], wr=[t_pay])
                        P.dma("sp", payc1[256:288, :], kstc[0:32, 2, :], rd=[t_kst], wr=[t_pay])
                    P.flush()
                if stop == "P" and L == 1:
                    return nc
                t_gath = Trk()
                if L == 0:
                    P.allgather(payK0, payKG0, wr=[t_gath])
                    P.allgather(payV0, payVG0, wr=[t_gath])
                else:
                    P.allgather(pay1, payG1, wr=[t_gath])
                    P.allgather(pay1r, payG1r, wr=[t_gath])

                if stop == "G" and L == 1:
                    return nc
                wout_pre = None
                if L == 0:
                    wout_pre = sb(sl, "wout_pre", [128, NCH, D], BF16)
                    t_wo_pre = Trk()
                    for c in range(NCH):
                        P.dma("pool", wout_pre[:, c, :], w_out[L][c * 128:(c + 1) * 128, :], wr=[t_wo_pre])
                with ExitStack() as sa:
                    NH2 = 4 if L == 0 else 2
                    Vb = sb(sa, f"V{L}", [128, NKT, NH2, 65], BF16)
                    if L == 0:
                        KT = sb(sa, "KT", [128, 2, NKEY], BF16)
                    else:
                        KT = sb(sa, "KTg", [128, 2, NKEY], BF16)
                        kvnT = sb(sa, "kvnT", [128, 2, NKEY], BF16)
                        QTg = sb(sa, "QTg", [128, 2, QC], BF16)
                        wkvbp = [sb(sa, f"wkvbp{i}", [128, 2, 256], BF16) for i in range(2)]
                        wqbp = [sb(sa, f"wqbp{i}", [128, 3, 192], BF16) for i in range(2)]
                        t_wp = [Trk(), Trk()]
                        xnrm = sb(sa, "xnrm", [128, 2 * 384 + 6 * 256], F32)
                        xt1 = sb(sa, "xt1", [128, 2 * 384 + 6 * 256], F32)
                        xt2 = sb(sa, "xt2", [128, 2 * 384], F32)
                        xss = sb(sa, "xss", [128, 32], F32)
                        xrs = sb(sa, "xrs", [128, 32], F32)
                        rope1q = sb(sa, "rope1q", [128, NT, 64], F32)
                        t_kvn = Trk()
                        t_kvnc = [Trk() for _ in range(5)]
                    vstg = sb(sa, "vstg", [128, NT, 256], BF16) if L == 0 else None
                    NPB = 3
                    pt = [sb(sa, f"pt{i}", [128, 2, 512], BF16) for i in range(NPB)]
                    osb = [sb(sa, f"osb{i}", [66, 512], F32) for i in range(2)]
                    rinv = sb(sa, "rinv", [128, 8], F32)
                    st = [ps(sa, f"st{i}", [128, 2, 512], F32) for i in range(2)]
                    ot = [ps(sa, f"ot{i}", [128, 512], F32) for i in range(2)]
                    tpo = ps(sa, "tpo", [128, 4, 128], F32)
                    tpk = ps(sa, "tpk", [128, 4, 128], F32)
                    t_tpk = Trk()
                    t_V, t_KT, t_vstg = Trk(), Trk(), Trk()
                    t_KTc = [Trk() for _ in range(5)]
                    t_Vc = [Trk() for _ in range(5)]
                    chunk = lambda kt: 0 if kt < 2 else 1 + (kt - 2) // NT
                    t_st, t_pt = [[Trk(), Trk()], [Trk(), Trk()]], [Trk() for _ in range(NPB)]
                    t_ot, t_osb, t_tpo, t_rinv = [Trk(), Trk()], [Trk(), Trk()], Trk(), Trk()
                    P.op("pool", lambda h: h.memset(Vb[:, :, :, 64:65], 1.0), wr=[t_V] + t_Vc)
                    if L == 0:
                        vstg2 = sb(sa, "vstg2", [128, NT, 256], BF16)
                        vst_ = [(vstg, t_vstg), (vstg2, Trk())]
                        for r in range(-1, 4):
                            sg_, t_sg = vst_[(r + 1) % 2]
                            if r < 0:
                                for j in range(2):
                                    P.dma("sp", KT[:, j, 0:CTX], paycK0[j * 128:(j + 1) * 128, :], wr=[t_KTc[0]])
                                P.dma("sp", sg_[:, 0:2, :], paycV0[:, :].rearrange("(t p) f -> p t f", p=128), wr=[t_sg])
                                n_, k0 = 2, 0
                            else:
                                for j in range(2):
                                    P.dma("sp", KT[:, j, CTX + r * TOK:CTX + (r + 1) * TOK],
                                          payKG0[r * 256 + j * 128:r * 256 + (j + 1) * 128, :], rd=[t_gath], wr=[t_KTc[1 + r]])
                                P.dma("sp", sg_[:, :, :], payVG0[r * TOK:(r + 1) * TOK, :].rearrange("(t p) f -> p t f", p=128),
                                      rd=[t_gath], wr=[t_sg])
                                n_, k0 = NT, 2 + r * NT
                            P.op("dve", lambda h, n_=n_, k0=k0, sg_=sg_: h.tensor_copy(
                                out=Vb[:, k0:k0 + n_, :, 0:64], in_=sg_[:, 0:n_, :].rearrange("p t (h d) -> p t h d", h=4)),
                                rd=[t_sg], wr=[t_Vc[r + 1]])
                    else:
                        t_w = Trk()
                        for c in range(2):
                            P.dma("sp", kvnT[:, c, 0:CTX], payc1[c * 128:(c + 1) * 128, :], rd=[t_gath], wr=[t_kvnc[0]])
                        for r in range(4):
                            for c in range(2):
                                P.dma("sp", kvnT[:, c, CTX + r * TOK:CTX + (r + 1) * TOK],
                                      payG1[r * 256 + c * 128:r * 256 + (c + 1) * 128, :], rd=[t_gath], wr=[t_kvnc[1 + r]])
                        P.dma("sp", rope1q[:, :, :], rope1[:, :, :], wr=[t_w])
                        for hl in range(2):
                            P.dma("sp", KT[64:96, hl, 0:CTX], payc1[256:288, :], rd=[t_gath], wr=[t_KT])
                            for r in range(4):
                                P.dma("sp", KT[64:96, hl, CTX + r * TOK:CTX + (r + 1) * TOK],
                                      payG1r[r * 32:(r + 1) * 32, :], rd=[t_gath], wr=[t_KT])

                    if PRE and L == 0:
                        t_pre = Trk()
                        for dst_, src_, rows, cols in ((wmod1_bf, w_mod[1], D, 3 * D), (win1_bf, w_in[1], D, 1696),
                                                       (wkvb_bf, w_kvb, 256, 2048), (wqb_bf, w_qb, 384, 1536),
                                                       (wout1_bf, w_out[1], D, D)):
                            for r0 in range(0, rows, 128):
                                for cb in range(0, cols, 1024):
                                    ce = min(cols, cb + 1024)
                                    P.dma("pool", dst_[r0:r0 + 128, cb:ce], src_[r0:r0 + 128, cb:ce], rd=[t_KTc[4], t_Vc[4]], wr=[t_pre])
                    if stop == "AL" and L == 1:
                        P.flush()
                        return nc

                    def attend(kA, kB, qA, qB, vA, vB, kts, qlen, scale, fin):
                        n = len(kts)

                        def qk(i):
                            b = i % 2
                            P.op("pe", lambda h: h.matmul(st[b][:, 0, 0:qlen], kA(kts[i]), qA, start=True, stop=True),
                                 rd=[t_KT, t_KTc[chunk(kts[i])], t_QT], wr=[t_st[b][0]], inc=False)
                            P.op("pe", lambda h: h.matmul(st[b][:, 1, 0:qlen], kB(kts[i]), qB, start=True, stop=True),
                                 rd=[t_KT, t_KTc[chunk(kts[i])], t_QT], wr=[t_st[b][1]])

                        def ex(i):
                            b, s_ = i % 2, i % NPB
                            P.op("act", lambda h: h.activation(out=pt[s_][:, :, 0:qlen], in_=st[b][:, :, 0:qlen],
                                                               func=AF.Exp, scale=scale),
                                 rd=t_st[b], wr=[t_pt[s_]])

                        def pv(i):
                            s_ = i % NPB
                            P.op("pe", lambda h: h.matmul(ot[0][0:65, 0:qlen], vA(kts[i]), pt[s_][:, 0, 0:qlen],
                                                          start=(i == 0), stop=(i == n - 1)),
                                 rd=[t_V, t_Vc[chunk(kts[i])], t_pt[s_]], wr=[t_ot[0]], inc=False)
                            P.op("pe", lambda h: h.matmul(ot[1][0:65, 0:qlen], vB(kts[i]), pt[s_][:, 1, 0:qlen],
                                                          start=(i == 0), stop=(i == n - 1)),
                                 rd=[t_V, t_Vc[chunk(kts[i])], t_pt[s_]], wr=[t_ot[1]])
                        def prologue():
                            qk(0)
                            if n > 1:
                                qk(1)

                        def body():
                            for i in range(n):
                                ex(i)
                                if i + 2 < n:
                                    qk(i + 2)
                                pv(i)
                        nq = qlen // 128

                        def finalise():
                            _finalise(nq, qlen, fin)
                        return prologue, body, finalise

                    def _finalise(nq, qlen, fin):
                        for hh in range(2):
                            P.op("dve", lambda h, hh=hh: h.tensor_copy(out=osb[hh][0:65, 0:qlen], in_=ot[hh][0:65, 0:qlen]),
                                 rd=[t_ot[hh]], wr=[t_osb[hh]])
                        for hh in range(2):
                            for qi in range(nq):
                                P.op("pe", lambda h, hh=hh, qi=qi: h.transpose(
                                    tpo[:, qi, 0:66], osb[hh][:, qi * 128:(qi + 1) * 128], identf[0:66, 0:66]),
                                    rd=[t_osb[hh], t_const], wr=[t_tpo], inc=(qi == nq - 1))
                            P.op("dve", lambda h, hh=hh: h.reciprocal(out=rinv[:, hh * 4:hh * 4 + nq], in_=tpo[:, 0:nq, 64]),
                                 rd=[t_tpo], wr=[t_rinv])
                            for qi in range(nq):
                                fin(hh, qi, tpo[:, qi, 0:64], rinv[:, hh * 4 + qi:hh * 4 + qi + 1])

                    def run_blocks(blocks):
                        for bi, (pro, body, fin_) in enumerate(blocks):
                            if bi == 0:
                                pro()
                            body()
                            if bi + 1 < len(blocks):
                                blocks[bi + 1][0]()
                            fin_()

                    def mk_fin(tile0, colA, colB):
                        def fin(hh, qi, o_ap, r_ap):
                            ti = tile0 + qi
                            col = colA if hh == 0 else colB
                            P.op("dve", lambda h: h.scalar_tensor_tensor(
                                out=gsb[:, ti, col:col + 64], in0=o_ap, scalar=r_ap, in1=gsb[:, ti, col:col + 64],
                                op0=ALU.mult, op1=ALU.mult), rd=[t_tpo, t_rinv, t_g[ti]], wr=[t_g[ti]])
                        return fin

                    if L == 0:
                        sc0 = 64 ** -0.5
                        blocks0 = []
                        for a in range(2):
                            for i in range(4):
                                slot = 4 * a + i
                                hA, hB = 8 * a + i, 8 * a + 4 + i
                                kA = lambda kt, a=a: KT[0:64, a, kt * 128:(kt + 1) * 128]
                                kB = lambda kt, a=a: KT[64:128, a, kt * 128:(kt + 1) * 128]
                                vA = lambda kt, a=a: Vb[:, kt, 2 * a, :]
                                vB = lambda kt, a=a: Vb[:, kt, 2 * a + 1, :]
                                for qb in range(4):
                                    blocks0.append(attend(kA, kB, QT[0:64, slot, qb * 512:(qb + 1) * 512],
                                                          QT[64:128, slot, qb * 512:(qb + 1) * 512], vA, vB,
                                                          list(range(NKT)), 512, sc0, mk_fin(qb * 4, hA * 64, hB * 64)))
                                blocks0.append(attend(kA, kB, QT[0:64, slot, TOK:TOK + CTX], QT[64:128, slot, TOK:TOK + CTX], vA, vB,
                                                      [0, 1], 256, sc0, mk_fin(NT, hA * 64, hB * 64)))
                        run_blocks(blocks0)
                    else:
                        sc1 = 96 ** -0.5
                        t_x = {n_: Trk() for n_ in ("scr", "nrm", "t1", "t2", "kb", "ss")}

                        def rstd_pool(G, n):
                            P.op("act", lambda h: h.activation(out=xrs[:, 0:G], in_=xss[:, 0:G], func=AF.Sqrt,
                                                               bias=epsb[:, 0:1], scale=1.0 / n),
                                 rd=[t_x["ss"], t_const], wr=[t_x["ss"]])
                            P.op("dve", lambda h: h.reciprocal(out=xrs[:, 0:G], in_=xrs[:, 0:G]), rd=[t_x["ss"]], wr=[t_x["ss"]])

                        def jam(gens):
                            gens = list(gens)
                            while gens:
                                for g_ in list(gens):
                                    try:
                                        next(g_)
                                    except StopIteration:
                                        gens.remove(g_)

                        NBIG = 2
                        banks = [(st[0][:, 0, :], t_st[0][0]), (st[0][:, 1, :], t_st[0][1]), (st[1][:, 0, :], t_st[1][0]),
                                 (st[1][:, 1, :], t_st[1][1]), (ot[0][:, :], t_ot[0]), (ot[1][:, :], t_ot[1]),
                                 (tpo[:, :, :].rearrange("p g k -> p (g k)"), t_tpo), (tpk[:, :, :].rearrange("p g k -> p (g k)"), t_tpk)]

                        def mkset(s_):
                            big = s_ < NBIG
                            o = s_ * 384 if big else NBIG * 384 + (s_ - NBIG) * 256
                            w_ = 384 if big else 256
                            bk, t_bk = banks[s_]
                            T = {n_: Trk() for n_ in ("nrm", "t1", "t2", "ss")}
                            T["tk"] = t_bk
                            return dict(nrm=xnrm[:, o:o + w_], t1=xt1[:, o:o + w_], t2=(xt2[:, s_ * 384:(s_ + 1) * 384] if big else None),
                                        ss=xss[:, s_ * 4:s_ * 4 + 4], rs=xrs[:, s_ * 4:s_ * 4 + 4],
                                        tk=bk.rearrange("p (g k) -> p g k", g=4), s=s_, pb=bk, tb=t_bk, big=big, T=T)
                        sets = [mkset(i_) for i_ in range(8)]

                        def roll(factories):
                            free, active, pending = list(range(len(sets))), [], list(factories)
                            while pending or active:
                                for pi, (need_big, fn) in enumerate(pending):
                                    if need_big:
                                        cand = [i_ for i_ in free if sets[i_]["big"]]
                                    else:
                                        cand = [i_ for i_ in free if not sets[i_]["big"]]
                                        if not cand and not any(nb for nb, _ in pending):
                                            cand = list(free)
                                    if cand:
                                        free.remove(cand[0])
                                        active.append((fn(sets[cand[0]]), cand[0]))
                                        pending.pop(pi)
                                        break
                                for item in list(active):
                                    try:
                                        next(item[0])
                                    except StopIteration:
                                        active.remove(item)
                                        free.append(item[1])

                        def load_pair_w(p):
                            qn_, kvs_, qbs_ = ("sp", wkvb_bf, wqb_bf) if PRE else ("pool", w_kvb, w_qb)
                            for c in range(2):
                                P.dma(qn_, wkvbp[p % 2][:, c, :], kvs_[c * 128:(c + 1) * 128, p * 256:(p + 1) * 256], wr=[t_wp[p % 2]])
                            for c in range(3):
                                P.dma(qn_, wqbp[p % 2][:, c, :], qbs_[c * 128:(c + 1) * 128, p * 192:(p + 1) * 192], wr=[t_wp[p % 2]])

                        def rstd_(S_, G, n):
                            T = S_["T"]
                            P.op("act", lambda h: h.activation(out=S_["rs"][:, 0:G], in_=S_["ss"][:, 0:G], func=AF.Sqrt,
                                                               bias=epsb[:, 0:1], scale=1.0 / n), rd=[T["ss"], t_const], wr=[T["ss"]])
                            P.op("dve", lambda h: h.reciprocal(out=S_["rs"][:, 0:G], in_=S_["rs"][:, 0:G]), rd=[T["ss"]], wr=[T["ss"]])

                        wcomb = sb(sa, "wcomb", [128, 96], F32)
                        P.op("dve", lambda h: h.tensor_copy(out=wcomb[:, :], in_=rep_sb[:, R_QN1:R_QN1 + 96]), rd=[t_const], wr=[t_w])
                        P.op("dve", lambda h: h.tensor_tensor(out=wcomb[:, 0:64], in0=wcomb[:, 0:64], in1=rep_sb[:, R_KNN:R_KNN + 64],
                                                              op=ALU.mult), rd=[t_const, t_w], wr=[t_w])

                        def qprep(p, tb, S_):
                            T = S_["T"]
                            pb = S_["pb"][:, 0:384].rearrange("p (u x) -> p u x", u=2)
                            pb4 = S_["pb"][:, 0:384].rearrange("p (g n) -> p g n", g=4)
                            t_pb = S_["tb"]
                            for u in range(2):
                                ti = 2 * tb + u
                                for c in range(3):
                                    P.op("pe", lambda h, u=u, ti=ti, c=c: h.matmul(
                                        pb[:, u, :], qanT[:, c, ti * 128:(ti + 1) * 128], wqbp[p % 2][:, c, :],
                                        start=(c == 0), stop=(c == 2)), rd=[t_QT, t_wp[p % 2]], wr=[t_pb], inc=(c == 2))
                            yield
                            g4 = lambda t_: t_[:, 0:384].rearrange("p (g n) -> p g n", g=4)
                            P.op("act", lambda h: h.activation(out=S_["t1"], in_=S_["pb"][:, 0:384], func=AF.Square),
                                 rd=[t_pb], wr=[T["t1"]])
                            yield
                            P.op("dve", lambda h: h.tensor_reduce(out=S_["ss"], in_=g4(S_["t1"]), axis=AX.X, op=ALU.add),
                                 rd=[T["t1"]], wr=[T["ss"]])
                            yield
                            rstd_(S_, 4, 96)
                            yield
                            P.op("dve", lambda h: h.tensor_tensor(out=g4(S_["nrm"]), in0=pb4, in1=bc(S_["rs"], [128, 4, 96], 2),
                                                                  op=ALU.mult), rd=[t_pb, T["ss"]], wr=[T["nrm"]])
                            yield
                            P.op("dve", lambda h: h.tensor_tensor(out=g4(S_["nrm"]), in0=g4(S_["nrm"]),
                                                                  in1=bc(wcomb[:, :], [128, 4, 96], 1), op=ALU.mult),
                                 rd=[T["nrm"], t_w], wr=[T["nrm"]])
                            yield
                            for u in range(2):
                                ti = 2 * tb + u
                                d3 = S_["nrm"][:, u * 192:(u + 1) * 192].rearrange("p (g n) -> p g n", g=2)[:, :, 64:96]
                                rope(S_["nrm"][:, u * 192:(u + 1) * 192], 2, 96, rope1q[:, ti, :],
                                     d3.rearrange("p g (a b q) -> p g a b q", a=2, b=2),
                                     S_["t1"], S_["t2"], T["nrm"], t_w, T["nrm"], T["t1"], T["t2"], off=64, rn=32)
                                yield
                            for g in range(4):
                                P.op("pe", lambda h, g=g: h.transpose(S_["tk"][0:96, g, :], S_["nrm"][:, g * 96:(g + 1) * 96], identf[:, :]),
                                     rd=[T["nrm"], t_const], wr=[T["tk"]], inc=(g == 3))
                            yield
                            P.op("act", lambda h: h.activation(
                                out=QTg[0:96, :, tb * 256:(tb + 1) * 256].rearrange("p h (u k) -> p h u k", u=2),
                                in_=S_["tk"][0:96, 0:4, :].rearrange("p (u h) k -> p h u k", h=2), func=AF.Copy),
                                rd=[T["tk"]], wr=[t_QT])
                            yield

                        def expand(p, kt0, S_):
                            T = S_["T"]
                            e3 = S_["pb"].rearrange("p (u x) -> p u x", u=2)
                            t_pb = S_["tb"]
                            for u in range(2):
                                for c in range(2):
                                    P.op("pe", lambda h, u=u, c=c: h.matmul(
                                        e3[:, u, :], kvnT[:, c, (kt0 + u) * 128:(kt0 + u + 1) * 128],
                                        wkvbp[p % 2][:, c, :], start=(c == 0), stop=(c == 1)),
                                        rd=[t_kvn, t_kvnc[chunk(kt0)], t_wp[p % 2]], wr=[t_pb], inc=(c == 1))
                            yield
                            e5 = e3.rearrange("p u (h x d) -> p u h x d", h=2, x=2)
                            v4 = lambda t_: t_[:, 0:256].rearrange("p (u h d) -> p u h d", u=2, h=2)
                            v3 = lambda t_: t_[:, 0:256].rearrange("p (g d) -> p g d", d=64)
                            P.op("act", lambda h: h.activation(out=v4(S_["t1"]), in_=e5[:, :, :, 0, :], func=AF.Square),
                                 rd=[t_pb], wr=[T["t1"]])
                            P.op("act", lambda h: h.activation(out=Vb[:, kt0:kt0 + 2, :, 0:64], in_=e5[:, :, :, 1, :], func=AF.Copy),
                                 rd=[t_pb], wr=[t_V])
                            yield
                            P.op("dve", lambda h: h.tensor_reduce(out=S_["ss"], in_=v3(S_["t1"]), axis=AX.X, op=ALU.add),
                                 rd=[T["t1"]], wr=[T["ss"]])
                            yield
                            rstd_(S_, 4, 64)
                            yield
                            P.op("dve", lambda h: h.tensor_tensor(out=v4(S_["nrm"]), in0=e5[:, :, :, 0, :],
                                                                  in1=bc(S_["rs"].rearrange("p (u h) -> p u h", u=2), [128, 2, 2, 64], 3),
                                                                  op=ALU.mult), rd=[t_pb, T["ss"]], wr=[T["nrm"]])
                            yield
                            for g in range(4):
                                P.op("pe", lambda h, g=g: h.transpose(S_["tk"][0:64, g, :], S_["nrm"][:, g * 64:(g + 1) * 64], identf[:, :]),
                                     rd=[T["nrm"], t_const], wr=[T["tk"]], inc=(g == 3))
                            yield
                            P.op("act", lambda h: h.activation(
                                out=KT[0:64, :, kt0 * 128:(kt0 + 2) * 128].rearrange("p h (u k) -> p h u k", u=2),
                                in_=S_["tk"][0:64, 0:4, :].rearrange("p (u h) k -> p h u k", h=2), func=AF.Copy),
                                rd=[T["tk"]], wr=[t_KT])
                            yield

                        load_pair_w(0)
                        for p in range(8):
                            if p + 1 < 8:
                                load_pair_w(p + 1)
                            roll([(True, (lambda S_, tb=tb, p=p: qprep(p, tb, S_))) for tb in range(NT // 2)] +
                                 [(False, (lambda S_, kt0=kt0, p=p: expand(p, kt0, S_))) for kt0 in range(0, NKT, 2)])
                            kA = lambda kt: KT[0:96, 0, kt * 128:(kt + 1) * 128]
                            kB = lambda kt: KT[0:96, 1, kt * 128:(kt + 1) * 128]
                            vA = lambda kt: Vb[:, kt, 0, :]
                            vB = lambda kt: Vb[:, kt, 1, :]
                            run_blocks([attend(kA, kB, QTg[0:96, 0, qb * 512:(qb + 1) * 512], QTg[0:96, 1, qb * 512:(qb + 1) * 512],
                                               vA, vB, list(range(NKT)), 512, sc1, mk_fin(qb * 4, (2 * p) * 64, (2 * p + 1) * 64))
                                        for qb in range(4)])
                    P.flush()
                with ExitStack() as se:
                    wout = wout_pre if wout_pre is not None else sb(se, f"wout{L}", [128, NCH, D], BF16)
                    ogT = [sb(se, f"ogT{i}", [128, NCH, 128], BF16) for i in range(2)]
                    xs2 = [sb(se, f"xs2{i}", [128, D], F32) for i in range(2)]
                    xo = [sb(se, f"xo{i}", [128, D], F32) for i in range(2)]
                    tpe = [ps(se, f"tpe{i}", [128, NCH, 128], BF16) for i in range(2)]
                    ps_y = [ps(se, f"ps_y{i}", [128, D], F32) for i in range(2)]
                    t_wo = Trk()
                    t_og, t_xs2, t_xo, t_py, t_tpe = ([Trk(), Trk()] for _ in range(5))
                    if wout_pre is None:
                        for c in range(NCH):
                            if L == 1 and PRE:
                                P.dma("sp", wout[:, c, :], wout1_bf[c * 128:(c + 1) * 128, :], wr=[t_wo])
                            else:
                                P.dma("pool", wout[:, c, :], w_out[L][c * 128:(c + 1) * 128, :], wr=[t_wo])
                    ada_next = None
                    if L == 0 and 1 in layers:
                        ld_, ada_next = adaln(1, se)
                        ld_()

                    def etile(ti):
                        isctx = ti >= NT
                        cond = 1 if isctx else 0
                        s = ti % 2
                        if L == 0:
                            src_ = ctx_b[(ti - NT) * 128:(ti - NT + 1) * 128, :] if isctx else x_own[ti * 128:(ti + 1) * 128, :]
                            dst_ = ctx1s[(ti - NT) * 128:(ti - NT + 1) * 128, :] if isctx else x1s[ti * 128:(ti + 1) * 128, :]
                        else:
                            src_ = x1s[ti * 128:(ti + 1) * 128, :]
                            dst_ = out_d[ti * 128:(ti + 1) * 128, :]
                        P.dma("sp", xs2[s][:, :], src_, wr=[t_xs2[s]])
                        for c in range(NCH):
                            P.op("pe", lambda h, c=c: h.transpose(tpe[s][:, c, :], gsb[:, ti, c * 128:(c + 1) * 128], ident[:, :]),
                                 rd=[t_g[ti], t_const], wr=[t_tpe[s]], inc=(c == NCH - 1))
                        yield
                        P.op("act", lambda h: h.activation(out=ogT[s][:, :, :], in_=tpe[s][:, :, :], func=AF.Copy),
                             rd=[t_tpe[s]], wr=[t_og[s]])
                        yield
                        for half in range(2):
                            for c in range(NCH):
                                P.op("pe", lambda h, c=c, half=half: h.matmul(
                                    ps_y[s][:, half * 512:(half + 1) * 512], ogT[s][:, c, :],
                                    wout[:, c, half * 512:(half + 1) * 512], start=(c == 0), stop=(c == NCH - 1)),
                                    rd=[t_og[s], t_wo], wr=[t_py[s]], inc=(c == NCH - 1))
                        yield
                        P.op("dve", lambda h: h.tensor_tensor(out=xo[s][:, :], in0=ps_y[s][:, :], in1=gate[L][:, cond, :], op=ALU.mult),
                             rd=[t_py[s], t_const], wr=[t_xo[s]])
                        yield
                        P.op("pool", lambda h: h.tensor_tensor(out=xo[s][:, :], in0=xo[s][:, :], in1=xs2[s][:, :], op=ALU.add),
                             rd=[t_xo[s], t_xs2[s]], wr=[t_xo[s]])
                        yield
                        P.dma("sp", dst_, xo[s][:, :], rd=[t_xo[s]])
                        yield

                    def ejam(gens):
                        gens = list(gens)
                        while gens:
                            for g_ in list(gens):
                                try:
                                    next(g_)
                                except StopIteration:
                                    gens.remove(g_)
                    for ti in range(0, NQT, 2):
                        ejam([etile(ti), etile(ti + 1)])
                    if ada_next is not None:
                        ada_next()
                    P.flush()
    return nc


def _rope_table(rot_dim, tok):
    row = (tok // 64).astype(np.float32)
    col = (tok % 64).astype(np.float32)
    axis_dim = rot_dim // 2
    inv = np.power(np.float32(10000.0), -np.arange(0, axis_dim, 2, dtype=np.float32) / np.float32(axis_dim)).astype(np.float32)
    ar = row[:, None] * inv[None, :]
    ac = col[:, None] * inv[None, :]
    ang = np.concatenate([ar, ar, ac, ac], axis=-1).astype(np.float32)
    q = rot_dim // 4
    sign = np.concatenate([-np.ones(q), np.ones(q), -np.ones(q), np.ones(q)]).astype(np.float32)
    return np.concatenate([np.cos(ang), np.sin(ang) * sign[None, :]], axis=-1).astype(np.float32)


def _colT(v, n):
    return np.ascontiguousarray(np.asarray(v, np.float32).reshape(n, 128).T)


def make_in_maps(inp):
    f = lambda k: np.ascontiguousarray(np.asarray(inp[k], np.float32))
    repv = np.concatenate([f("l0_b_mod")[2048:], f("l1_b_mod")[2048:], f("l0_q_norm"), f("l0_k_norm"),
                           f("l1_kv_a_norm"), f("l1_k_rope_norm"), f("l1_q_a_norm"), f("l1_q_norm"),
                           f("l1_k_nope_norm")]).astype(np.float32)
    assert repv.shape[0] == NREP
    rep = np.ascontiguousarray(np.broadcast_to(repv[None, :], (128, NREP)))
    ident = np.eye(128, dtype=np.float32)
    shared = {"rep": rep, "ident": ident.astype(ml_dtypes.bfloat16), "identf": ident,
              "l0_w_mod": f("l0_w_mod"), "l1_w_mod": f("l1_w_mod"), "l0_w_in": f("l0_w_in"), "l1_w_in": f("l1_w_in"),
              "l0_w_out": f("l0_w_out"), "l1_w_out": f("l1_w_out"), "l1_w_kv_b": f("l1_w_kv_b"), "l1_w_q_b": f("l1_w_q_b")}
    x, c, ctx, c_ctx = f("x"), f("c"), f("ctx"), f("c_ctx")
    maps = []
    for core in range(8):
        b, qt = core // 4, core % 4
        tok = qt * TOK + np.arange(TOK)
        m = dict(shared)
        m["x_own"] = np.ascontiguousarray(x[b, qt * TOK:(qt + 1) * TOK])
        m["ctx_b"] = np.ascontiguousarray(ctx[b])
        m["vecT"] = np.ascontiguousarray(np.concatenate(
            [_colT(c[b], 8), _colT(c_ctx, 8), _colT(f("l0_norm"), 8), _colT(f("l0_b_mod"), 24),
             _colT(f("l1_norm"), 8), _colT(f("l1_b_mod"), 24)], axis=1))
        for name, rd in (("rope0", 64), ("rope1", 32)):
            t = _rope_table(rd, tok)
            m[name] = np.ascontiguousarray(t.reshape(NT, 128, 2 * rd).transpose(1, 0, 2))
        maps.append(m)
    return maps


_NC_CACHE = {}


def kernel(**inputs):
    if "nc" not in _NC_CACHE:
        _NC_CACHE["nc"] = build()
    maps = make_in_maps(inputs)
    res = run_bass_kernel_spmd(_NC_CACHE["nc"], maps, core_ids=list(range(8)))
    out = np.empty((2, 4 * TOK, D), np.float32)
    for core in range(8):
        b, qt = core // 4, core % 4
        out[b, qt * TOK:(qt + 1) * TOK] = np.asarray(res.results[core]["out"], np.float32)
    return out
```
